# Optimizing a Trainium2 kernel written in Bass

```python
import jax, jax.numpy as jnp
from jax import lax
import numpy as np

D_MODEL = 1024
BATCH = 4
SEQ = 8192
DEPTH = 2

GRID_W = 64
HEAD_DIM = 64
Q_BLOCK = 128
ROPE_THETA = 10000.0
EPS = 1e-6
A_HEADS = 8
A_KV_HEADS = 2
B_HEADS = 8
B_Q_RANK = 384
B_KV_RANK = 256
B_NOPE = 64
B_ROPE = 32
B_V = 64
C_HEADS = 16
C_WIN_ROWS = 8
C_WIN_COLS = 16
D_FF = 4 * D_MODEL

A_Q = A_HEADS * HEAD_DIM
A_KV = A_KV_HEADS * HEAD_DIM
EVEN_SPLITS = (A_Q, A_Q + A_KV, A_Q + 2 * A_KV, A_Q + 2 * A_KV + B_Q_RANK,
               A_Q + 2 * A_KV + B_Q_RANK + B_KV_RANK)
EVEN_IN = A_Q + 2 * A_KV + B_Q_RANK + B_KV_RANK + B_ROPE
MIX_WIDTH = A_HEADS * HEAD_DIM + B_HEADS * B_V
C_WIDTH = C_HEADS * HEAD_DIM
N_EVEN = (DEPTH + 1) // 2
N_ODD = DEPTH // 2

kernel_name = "hybrid_gqa_mla_natten_encoder"


def rmsnorm(x, g):
    xf = x.astype(jnp.float32)
    y = xf * lax.rsqrt(jnp.mean(xf * xf, axis=-1, keepdims=True) + EPS)
    return (y * g.astype(jnp.float32)).astype(x.dtype)


def rope_1d(x, pos):
    d = x.shape[-1]
    inv = ROPE_THETA ** (-jnp.arange(0, d, 2, dtype=jnp.float32) / d)
    ang = pos[:, None] * inv[None, :]
    cos = jnp.cos(ang)[:, None, :].astype(x.dtype)
    sin = jnp.sin(ang)[:, None, :].astype(x.dtype)
    x1, x2 = jnp.split(x, 2, axis=-1)
    return jnp.concatenate([x1 * cos - x2 * sin, x2 * cos + x1 * sin], axis=-1)


def axial_rope(x, row, col):
    xr, xc = jnp.split(x, 2, axis=-1)
    return jnp.concatenate([rope_1d(xr, row), rope_1d(xc, col)], axis=-1)


def block_attention(q, k, v, scale):
    b, s, hq, d = q.shape
    hkv = k.shape[2]
    g = hq // hkv
    nblk = s // Q_BLOCK
    qb = q.reshape(b, nblk, Q_BLOCK, hkv, g, d).transpose(1, 0, 3, 4, 2, 5)
    kt = k.transpose(0, 2, 1, 3)
    vt = v.transpose(0, 2, 1, 3)

    def one(qblk):
        sc = jnp.einsum('bkgqd,bksd->bkgqs', qblk, kt).astype(jnp.float32) * scale
        p = jax.nn.softmax(sc, axis=-1).astype(vt.dtype)
        return jnp.einsum('bkgqs,bksd->bkgqd', p, vt)

    o = lax.map(one, qb)
    return o.transpose(1, 0, 4, 2, 3, 5).reshape(b, s, hq, -1)


def hybrid_attention(h, w_in, a_q_norm, a_k_norm, b_q_norm, b_w_uq, b_kv_norm, b_w_ukv,
                     w_out, row, col):
    b, s, _ = h.shape
    z = h @ w_in
    qa, ka, va, cq, ckv, kr = jnp.split(z, EVEN_SPLITS, axis=-1)
    qa = axial_rope(rmsnorm(qa.reshape(b, s, A_HEADS, HEAD_DIM), a_q_norm), row, col)
    ka = axial_rope(rmsnorm(ka.reshape(b, s, A_KV_HEADS, HEAD_DIM), a_k_norm), row, col)
    va = va.reshape(b, s, A_KV_HEADS, HEAD_DIM)
    oa = block_attention(qa, ka, va, HEAD_DIM ** -0.5)
    qb = (rmsnorm(cq, b_q_norm) @ b_w_uq).reshape(b, s, B_HEADS, B_NOPE + B_ROPE)
    q_nope, q_rope = jnp.split(qb, (B_NOPE,), axis=-1)
    q_rope = axial_rope(q_rope, row, col)
    kvb = (rmsnorm(ckv, b_kv_norm) @ b_w_ukv).reshape(b, s, B_HEADS, B_NOPE + B_V)
    k_nope, vb = jnp.split(kvb, (B_NOPE,), axis=-1)
    k_rope = axial_rope(kr[:, :, None, :], row, col)
    q_full = jnp.concatenate([q_nope, q_rope], axis=-1)
    k_full = jnp.concatenate([k_nope, jnp.broadcast_to(k_rope, (b, s, B_HEADS, B_ROPE))], axis=-1)
    ob = block_attention(q_full, k_full, vb, (B_NOPE + B_ROPE) ** -0.5)
    mixed = jnp.concatenate([oa.reshape(b, s, -1), ob.reshape(b, s, -1)], axis=-1)
    return mixed @ w_out


def neighbourhood_attention(h, w_qkv, rpb, w_out):
    b, s, _ = h.shape
    rows = s // GRID_W
    wr = min(C_WIN_ROWS, rows)
    wc = C_WIN_COLS
    nblk = s // Q_BLOCK
    q, k, v = jnp.split(h @ w_qkv, 3, axis=-1)
    q = q.reshape(b, s, C_HEADS, HEAD_DIM).transpose(0, 2, 1, 3)
    k = k.reshape(b, s, C_HEADS, HEAD_DIM).transpose(0, 2, 1, 3)
    v = v.reshape(b, s, C_HEADS, HEAD_DIM).transpose(0, 2, 1, 3)
    t = jnp.arange(s, dtype=jnp.int32)
    qr, qc = t // GRID_W, t % GRID_W
    rs = jnp.clip(qr - wr // 2, 0, rows - wr)
    cs = jnp.clip(qc - wc // 2, 0, GRID_W - wc)
    kr = rs[:, None, None] + jnp.arange(wr, dtype=jnp.int32)[None, :, None]
    kc = cs[:, None, None] + jnp.arange(wc, dtype=jnp.int32)[None, None, :]
    idx = (kr * GRID_W + kc).reshape(s, wr * wc)
    ridx = jnp.broadcast_to(kr - qr[:, None, None] + (C_WIN_ROWS - 1), (s, wr, wc)).reshape(s, -1)
    cidx = jnp.broadcast_to(kc - qc[:, None, None] + (C_WIN_COLS - 1), (s, wr, wc)).reshape(s, -1)
    qbk = q.reshape(b, C_HEADS, nblk, Q_BLOCK, HEAD_DIM).transpose(2, 0, 1, 3, 4)
    blk = lambda a: a.reshape(nblk, Q_BLOCK, -1)
    scale = HEAD_DIM ** -0.5

    def one(args):
        qblk, iblk, rblk, cblk = args
        kg = jnp.take(k, iblk, axis=2)
        vg = jnp.take(v, iblk, axis=2)
        bias = rpb[:, rblk, cblk].astype(jnp.float32)
        sc = jnp.einsum('bhqd,bhqkd->bhqk', qblk, kg).astype(jnp.float32) * scale + bias[None]
        p = jax.nn.softmax(sc, axis=-1).astype(vg.dtype)
        return jnp.einsum('bhqk,bhqkd->bhqd', p, vg)

    o = lax.map(one, (qbk, blk(idx), blk(ridx), blk(cidx)))
    o = o.transpose(1, 0, 3, 2, 4).reshape(b, s, C_WIDTH)
    return o @ w_out


def sq_relu_mlp(h, w_up, w_down):
    u = jax.nn.relu(h @ w_up)
    return (u * u) @ w_down


def setup_inputs(seed: int = 0) -> dict:
    key = jax.random.key(seed)
    ks = jax.random.split(key, 20)

    def w(k, shape, fan_in):
        return jax.random.normal(k, shape, jnp.float32) * (fan_in ** -0.5)

    def gain(k, shape):
        return 1.0 + 0.05 * jax.random.normal(k, shape, jnp.float32)

    return {
        "x": jax.random.normal(ks[0], (BATCH, SEQ, D_MODEL), jnp.float32),
        "norm_mix": gain(ks[1], (DEPTH, D_MODEL)),
        "ev_w_in": w(ks[2], (N_EVEN, D_MODEL, EVEN_IN), D_MODEL),
        "ev_a_q_norm": gain(ks[3], (N_EVEN, HEAD_DIM)),
        "ev_a_k_norm": gain(ks[4], (N_EVEN, HEAD_DIM)),
        "ev_b_q_norm": gain(ks[5], (N_EVEN, B_Q_RANK)),
        "ev_b_w_uq": w(ks[6], (N_EVEN, B_Q_RANK, B_HEADS * (B_NOPE + B_ROPE)), B_Q_RANK),
        "ev_b_kv_norm": gain(ks[7], (N_EVEN, B_KV_RANK)),
        "ev_b_w_ukv": w(ks[8], (N_EVEN, B_KV_RANK, B_HEADS * (B_NOPE + B_V)), B_KV_RANK),
        "ev_w_out": w(ks[9], (N_EVEN, MIX_WIDTH, D_MODEL), MIX_WIDTH),
        "od_w_qkv": w(ks[10], (N_ODD, D_MODEL, 3 * C_WIDTH), D_MODEL),
        "od_rpb": 0.1 * jax.random.normal(ks[11], (N_ODD, C_HEADS, 2 * C_WIN_ROWS - 1, 2 * C_WIN_COLS - 1), jnp.float32),
        "od_w_out": w(ks[12], (N_ODD, C_WIDTH, D_MODEL), C_WIDTH),
        "norm_ffn": gain(ks[13], (DEPTH, D_MODEL)),
        "ffn_w_up": w(ks[14], (DEPTH, D_MODEL, D_FF), D_MODEL),
        "ffn_w_down": w(ks[15], (DEPTH, D_FF, D_MODEL), D_FF),
        "final_norm": gain(ks[16], (D_MODEL,)),
    }


def reference(x, norm_mix, ev_w_in, ev_a_q_norm, ev_a_k_norm, ev_b_q_norm, ev_b_w_uq,
              ev_b_kv_norm, ev_b_w_ukv, ev_w_out, od_w_qkv, od_rpb, od_w_out,
              norm_ffn, ffn_w_up, ffn_w_down, final_norm):
    s = x.shape[1]
    t = jnp.arange(s, dtype=jnp.int32)
    row = (t // GRID_W).astype(jnp.float32)
    col = (t % GRID_W).astype(jnp.float32)
    h = x
    for layer in range(DEPTH):
        i = layer // 2
        hn = rmsnorm(h, norm_mix[layer])
        if layer % 2 == 0:
            h = h + hybrid_attention(hn, ev_w_in[i], ev_a_q_norm[i], ev_a_k_norm[i],
                                     ev_b_q_norm[i], ev_b_w_uq[i], ev_b_kv_norm[i],
                                     ev_b_w_ukv[i], ev_w_out[i], row, col)
        else:
            h = h + neighbourhood_attention(hn, od_w_qkv[i], od_rpb[i], od_w_out[i])
        h = h + sq_relu_mlp(rmsnorm(h, norm_ffn[layer]), ffn_w_up[layer], ffn_w_down[layer])
    return rmsnorm(h, final_norm)
```

```python
import contextlib
import numpy as np
import concourse.bass as bass
import concourse.mybir as mybir
from concourse.bass_utils import run_bass_kernel_spmd

F32 = mybir.dt.float32
BF16 = mybir.dt.bfloat16
AF = mybir.ActivationFunctionType
ALU = mybir.AluOpType
AX = mybir.AxisListType

COMPUTE = ("pe", "act", "dve", "pool")
ALLENG = COMPUTE + ("sp",)

S_ALL = 8192
NT_ALL = 64
NT_OWN = 34
NT_OUT = 32
TOK_OWN = NT_OWN * 128
TOK_OUT = NT_OUT * 128
EPS = 1e-6
NEG = -30000.0


class Op:
    __slots__ = ("id", "eng", "fn", "deps", "dma", "cum", "marked", "cnt")


class Prog:
    def __init__(self, nc, same_eng_sync=True):
        self.nc = nc
        self.same_eng_sync = same_eng_sync
        self.ops = []
        self.last_w = {}
        self.rd_eng = {}
        self.rd_dma = {}
        self.dma_cum = {}
        self.dma_last = {}
        self.last_on_eng = {}

    def add(self, eng, fn, reads=(), writes=(), dma=None, extra_deps=()):
        op = Op()
        op.id = len(self.ops)
        op.eng = eng
        op.fn = fn
        op.dma = dma
        op.marked = dma is not None
        op.cnt = 0
        op.cum = 0
        deps = set(extra_deps)
        for k in reads:
            w = self.last_w.get(k)
            if w is not None:
                deps.add(w)
        for k in writes:
            w = self.last_w.get(k)
            if w is not None:
                deps.add(w)
            for r in self.rd_eng.get(k, {}).values():
                deps.add(r)
            for r in self.rd_dma.get(k, ()):
                deps.add(r)
        d2 = set()
        for d in deps:
            o = self.ops[d]
            if o.dma is not None:
                d2.add(self.dma_last[o.dma])
            else:
                d2.add(d)
        op.deps = d2
        if dma is not None:
            self.dma_cum[dma] = self.dma_cum.get(dma, 0) + 16
            op.cum = self.dma_cum[dma]
            self.dma_last[dma] = op.id
        else:
            self.last_on_eng[eng] = op.id
        for k in writes:
            self.last_w[k] = op.id
            self.rd_eng[k] = {}
            self.rd_dma[k] = []
        for k in reads:
            if k in writes:
                continue
            if dma is not None:
                self.rd_dma.setdefault(k, []).append(op.id)
            else:
                self.rd_eng.setdefault(k, {})[eng] = op.id
        self.ops.append(op)
        return op

    def barrier(self):
        deps = set(self.last_on_eng.values()) | set(self.dma_last.values())
        for e in ALLENG:
            self.add(e, None, extra_deps=deps)
        self.last_w = {}
        self.rd_eng = {}
        self.rd_dma = {}

    def emit(self):
        nc = self.nc
        ops = self.ops
        ses = self.same_eng_sync

        def skip(o, op):
            return (o.dma is None and op.dma is None and o.eng == op.eng
                    and (o.eng == "pe" or not ses) and op.fn is not None)

        for op in ops:
            for d in op.deps:
                o = ops[d]
                if o.dma is not None or skip(o, op):
                    continue
                o.marked = True
        run = {e: 0 for e in ALLENG}
        for op in ops:
            if op.dma is None:
                if op.marked and op.fn is not None:
                    run[op.eng] += 1
                op.cnt = run[op.eng]
        per_eng = {e: [] for e in ALLENG}
        for op in ops:
            per_eng[op.eng].append(op)
        with contextlib.ExitStack() as st:
            sems = {e: st.enter_context(nc.semaphore("s_" + e)) for e in COMPUTE}
            dsems = {k: st.enter_context(nc.semaphore("d_%d" % i))
                     for i, k in enumerate(self.dma_cum)}
            block = st.enter_context(nc.Block())

            def run_engine(eng_name, eng):
                waited = {}
                for op in per_eng[eng_name]:
                    need = {}
                    for d in op.deps:
                        o = ops[d]
                        if o.dma is not None:
                            s = ("d", o.dma)
                            v = o.cum
                        else:
                            if skip(o, op):
                                continue
                            if o.fn is None:
                                continue
                            s = ("c", o.eng)
                            v = o.cnt
                        if v <= 0 or waited.get(s, 0) >= v:
                            continue
                        if need.get(s, 0) < v:
                            need[s] = v
                    for s, v in need.items():
                        sem = dsems[s[1]] if s[0] == "d" else sems[s[1]]
                        eng.wait_ge(sem, v)
                        waited[s] = v
                    if op.fn is None:
                        continue
                    ins = op.fn(eng)
                    if op.dma is not None:
                        ins.then_inc(dsems[op.dma], 16)
                    elif op.marked:
                        ins.then_inc(sems[op.eng], 1)
                for k, sem in dsems.items():
                    last = ops[self.dma_last[k]]
                    if last.eng == eng_name and waited.get(("d", k), 0) < self.dma_cum[k]:
                        eng.wait_ge(sem, self.dma_cum[k])

            @block.tensor
            def _(e):
                run_engine("pe", e)

            @block.scalar
            def _(e):
                run_engine("act", e)

            @block.vector
            def _(e):
                run_engine("dve", e)

            @block.gpsimd
            def _(e):
                run_engine("pool", e)

            @block.sync
            def _(e):
                run_engine("sp", e)


class Builder:
    def __init__(self, debug=False, phases="ABCDEF"):
        self.debug = debug
        self.phases = phases
        self.nc = bass.Bass("TRN2", target_bir_lowering=False)
        import os
        self.P = Prog(self.nc, same_eng_sync=not os.environ.get("K_NOSES"))
        self.uid = 0

    def reset_arena(self):
        self.off = self.base_off

    def T(self, shape, dt, parts=128):
        n = int(np.prod(shape))
        nbytes = n * (4 if dt == F32 else 2)
        nbytes = (nbytes + 63) // 64 * 64
        assert self.off + nbytes <= self.arena_bytes, ("SBUF arena overflow", self.off, nbytes)
        ap = self.arena[0:parts, self.off // 2:(self.off + nbytes) // 2]
        self.off += nbytes
        if dt == F32:
            ap = ap.bitcast(F32)
        ap = ap[:, 0:n]
        if len(shape) == 2:
            ap = ap.rearrange("p (a b) -> p a b", a=shape[0], b=shape[1])
        elif len(shape) == 3:
            ap = ap.rearrange("p (a b c) -> p a b c", a=shape[0], b=shape[1], c=shape[2])
        return ap

    def key(self, s):
        self.uid += 1
        return "%s#%d" % (s, self.uid)

    def dram(self, name, shape, dt, kind=None):
        if kind is None:
            kind = "ExternalOutput" if self.debug else "Internal"
        return self.nc.dram_tensor(name, list(shape), dt, kind=kind).ap()

    def mm(self, out, lhsT, rhs, start, stop, reads, writes):
        self.P.add("pe", lambda e: e.matmul(out, lhsT=lhsT, rhs=rhs, start=start, stop=stop),
                   reads, writes)

    def tr(self, out, in_, reads, writes):
        ident = self.ident
        npart = in_.shape[0]
        self.P.add("pe", lambda e: e.transpose(out=out, in_=in_, identity=ident[0:npart, 0:npart]),
                   list(reads) + ["ident"], writes)

    def act(self, out, in_, func, reads, writes, scale=1.0, accum_out=None):
        if accum_out is None:
            self.P.add("act", lambda e: e.activation(out=out, in_=in_, func=func, scale=scale),
                       reads, writes)
        else:
            self.P.add("act", lambda e: e.activation(out=out, in_=in_, func=func, scale=scale,
                                                     accum_out=accum_out), reads, writes)

    def tt(self, eng, out, in0, in1, op, reads, writes):
        self.P.add(eng, lambda e: e.tensor_tensor(out=out, in0=in0, in1=in1, op=op), reads, writes)

    def ts(self, eng, out, in0, s1, s2, op0, op1, reads, writes):
        if op1 is None:
            self.P.add(eng, lambda e: e.tensor_scalar(out=out, in0=in0, scalar1=s1, scalar2=None,
                                                      op0=op0), reads, writes)
        else:
            self.P.add(eng, lambda e: e.tensor_scalar(out=out, in0=in0, scalar1=s1, scalar2=s2,
                                                      op0=op0, op1=op1), reads, writes)

    def stt(self, eng, out, in0, scalar, in1, op0, op1, reads, writes):
        self.P.add(eng, lambda e: e.scalar_tensor_tensor(out=out, in0=in0, scalar=scalar, in1=in1,
                                                         op0=op0, op1=op1), reads, writes)

    def copy(self, eng, out, in_, reads, writes):
        if eng == "act":
            self.P.add("act", lambda e: e.copy(out=out, in_=in_), reads, writes)
        else:
            self.P.add(eng, lambda e: e.tensor_copy(out=out, in_=in_), reads, writes)

    def dma(self, out, in_, reads, writes, sem, q="sp"):
        self.P.add(q, lambda e: e.dma_start(out=out, in_=in_), reads, writes, dma=sem)

    def rstd(self, out, ssq, n, key):
        self.ts("dve", out, ssq, 1.0 / n, EPS, ALU.mult, ALU.add, [key], [key])
        self.act(out, out, AF.Sqrt, [key], [key])
        self.P.add("dve", lambda e: e.reciprocal(out=out, in_=out), [key], [key])

    def bcast_load(self, dst, src_1d, key, q="sp"):
        n = dst.shape[1]
        self.dma(dst, src_1d.partition_broadcast(128), [], [key], key, q=q)

    def wload(self, dst, src2d, key):
        self.dma(dst, src2d.rearrange("(k p) n -> p k n", p=128), [], [key], key, q="pool")

    def rms_to_T(self, xt, xkey, g_bc, gkey, hn, hnkey, hnT_dst, hnTkey, psb, pskey, par):
        junk, ssq, rs = self.junk, self.ssq_x[par], self.rs_x[par]
        self.act(junk, xt, AF.Square, [xkey], ["junk", "ssqx%d" % par], accum_out=ssq)
        k = "ssqx%d" % par
        self.ts("dve", rs, ssq, 1.0 / 1024, EPS, ALU.mult, ALU.add, [k], ["rsx%d" % par])
        self.act(rs, rs, AF.Sqrt, ["rsx%d" % par], ["rsx%d" % par])
        self.P.add("dve", lambda e: e.reciprocal(out=rs, in_=rs), ["rsx%d" % par], ["rsx%d" % par])
        self.stt("dve", hn, xt, rs, g_bc, ALU.mult, ALU.mult, [xkey, "rsx%d" % par, gkey], [hnkey])
        for c in range(8):
            self.tr(psb[:, c * 128:(c + 1) * 128], hn[:, c * 128:(c + 1) * 128], [hnkey], [pskey])
        self.copy("act", hnT_dst, psb[:, 0:1024].rearrange("p (c t) -> p c t", c=8), [pskey], [hnTkey])

    def rope(self, dst_bf, src, C, S, nh, d, tmp1, tmp2, rk, skey, dkey, wide=None):
        q = d // 4
        e0 = "dve" if skey.startswith("ps") else "pool"
        S5 = S.rearrange("p (b f e) -> p b f e", b=2, f=2, e=q)
        if nh == 1:
            src, dst_bf, tmp1, tmp2 = src[:, 0, :], dst_bf[:, 0, :], tmp1[:, 0, :], tmp2[:, 0, :]
            self.tt(e0, tmp1, src, C, ALU.mult, rk + [skey], ["rt1"])
            s5 = src.rearrange("p (b f e) -> p b f e", b=2, f=2, e=q)
            t5 = tmp2.rearrange("p (b f e) -> p b f e", b=2, f=2, e=q)
            for f in range(2):
                self.tt("dve", t5[:, :, f, :], s5[:, :, 1 - f, :], S5[:, :, f, :], ALU.mult, rk + [skey], ["rt2"])
            self.tt("dve", tmp1, tmp1, tmp2, ALU.add, ["rt1", "rt2"], ["rt1"])
            if wide is not None:
                self.copy("dve", wide[0], wide[1], ["rt1"], [dkey])
            else:
                self.copy("dve", dst_bf, tmp1, ["rt1"], [dkey])
            return
        Cb = C.unsqueeze(1).to_broadcast([128, nh, d])
        self.tt(e0, tmp1, src, Cb, ALU.mult, rk + [skey], ["rt1"])
        s5 = src.rearrange("p h (b f e) -> p h b f e", b=2, f=2, e=q)
        t5 = tmp2.rearrange("p h (b f e) -> p h b f e", b=2, f=2, e=q)
        for b in range(2):
            for f in range(2):
                Sb = S5[:, b, f, :].unsqueeze(1).to_broadcast([128, nh, q])
                self.tt("dve", t5[:, :, b, f, :], s5[:, :, b, 1 - f, :], Sb, ALU.mult,
                        rk + [skey], ["rt2"])
        self.tt("dve", dst_bf, tmp1, tmp2, ALU.add, ["rt1", "rt2"], [dkey])

    def build(self):
        nc = self.nc
        P = self.P
        dbg = self.debug
        EI = "ExternalInput"
        self.x_d = nc.dram_tensor("x", [S_ALL, 1024], F32, kind=EI).ap()
        self.tab_d = nc.dram_tensor("tab", [S_ALL, 192], F32, kind=EI).ap()
        self.norm_mix = nc.dram_tensor("norm_mix", [2, 1024], F32, kind=EI).ap()
        self.norm_ffn = nc.dram_tensor("norm_ffn", [2, 1024], F32, kind=EI).ap()
        self.final_norm = nc.dram_tensor("final_norm", [1024], F32, kind=EI).ap()
        self.w_in = nc.dram_tensor("w_in", [1024, 1440], F32, kind=EI).ap()
        self.a_q_norm = nc.dram_tensor("a_q_norm", [64], F32, kind=EI).ap()
        self.a_k_norm = nc.dram_tensor("a_k_norm", [64], F32, kind=EI).ap()
        self.b_q_norm = nc.dram_tensor("b_q_norm", [384], F32, kind=EI).ap()
        self.b_kv_norm = nc.dram_tensor("b_kv_norm", [256], F32, kind=EI).ap()
        self.w_uq = nc.dram_tensor("w_uq", [384, 768], F32, kind=EI).ap()
        self.w_ukv = nc.dram_tensor("w_ukv", [256, 1024], F32, kind=EI).ap()
        self.w_out0 = nc.dram_tensor("w_out0", [1024, 1024], F32, kind=EI).ap()
        self.w_qkv = nc.dram_tensor("w_qkv", [1024, 3072], F32, kind=EI).ap()
        self.biasT = nc.dram_tensor("biasT", [3, 128, 16, 640], F32, kind=EI).ap()
        self.w_out1 = nc.dram_tensor("w_out1", [1024, 1024], F32, kind=EI).ap()
        self.w_up = nc.dram_tensor("w_up", [2, 1024, 4096], F32, kind=EI).ap()
        self.w_down = nc.dram_tensor("w_down", [2, 4096, 1024], F32, kind=EI).ap()
        self.out_d = nc.dram_tensor("out", [TOK_OUT, 1024], F32, kind="ExternalOutput").ap()
        self.kT_d = self.dram("kT", [10, 64, S_ALL], BF16)
        self.krT_d = self.dram("krT", [32, S_ALL], BF16)
        self.vE_d = self.dram("vE", [10, 128, NT_ALL, 65], BF16)
        self.qT_d = self.dram("qT", [16, 96, TOK_OWN], BF16)
        self.mix0_d = self.dram("mix0", [16, 64, TOK_OWN], BF16)
        self.h1_d = self.dram("h1", [TOK_OWN, 1024], F32)
        self.q1T_d = self.dram("q1T", [8, 128, TOK_OWN], BF16)
        self.k1T_d = self.dram("k1T", [8, 128, TOK_OWN], BF16)
        self.v1_d = self.dram("v1", [128, NT_OWN, 16, 65], BF16)
        self.mix1_d = self.dram("mix1", [16, 64, TOK_OWN], BF16)

        with contextlib.ExitStack() as st:
            self.arena_bytes = 206 * 1024
            self.arena = st.enter_context(nc.sbuf_tensor("arena", [128, self.arena_bytes // 2], BF16))
            self.ps = st.enter_context(nc.psum_tensor("ps", [128, 8, 512], F32))
            st.enter_context(nc.allow_low_precision("bf16 matmul operands, fp32 PSUM accumulation"))
            self.off = 0
            self.ident = self.T([128], BF16)
            self.identf = self.T([128], F32)
            self.ones = self.T([64], F32)
            self.junk = self.T([1024], F32)
            self.ssq_x = [self.T([1], F32) for _ in range(2)]
            self.rs_x = [self.T([1], F32) for _ in range(2)]
            self.base_off = self.off
            identf, ident, ones = self.identf, self.ident, self.ones
            P.add("pool", lambda e: e.memset(identf, 0.0), [], ["identf"])
            P.add("pool", lambda e: e.affine_select(out=identf, in_=identf, pattern=[[-1, 128]],
                                                    compare_op=ALU.not_equal, fill=1.0, base=0,
                                                    channel_multiplier=1), ["identf"], ["identf"])
            P.add("dve", lambda e: e.tensor_copy(out=ident, in_=identf), ["identf"], ["ident"])
            P.add("pool", lambda e: e.memset(ones, 1.0), [], ["ones"])
            P.barrier()
            if "A" in self.phases:
                self.phase_A()
                P.barrier()
            if "B" in self.phases:
                self.phase_B()
                P.barrier()
            if "C" in self.phases:
                self.phase_ffn(0)
                P.barrier()
            if "D" in self.phases:
                self.phase_D()
                P.barrier()
            if "E" in self.phases:
                self.phase_E()
                P.barrier()
            if "F" in self.phases:
                self.phase_ffn(1)
                P.barrier()
            P.emit()
        return nc

    def phase_A(self):
        P = self.P
        ps = self.ps
        self.reset_arena()
        T = self.T
        W1 = T([8, 512], BF16)
        W2 = T([8, 416], BF16)
        W3 = T([8, 512], BF16)
        Wuq = T([3, 768], BF16)
        Wuk = T([2, 512], BF16)
        Wuv = T([2, 512], BF16)
        g0 = T([1024], F32)
        gq = T([64], F32)
        gk = T([64], F32)
        gbq = T([384], F32)
        gbkv = T([256], F32)
        wv = self.w_in.rearrange("(k p) n -> p k n", p=128)
        self.dma(W1[:, :, 0:256], wv[:, :, 512:768], [], ["W1"], "W1a", q="pool")
        self.dma(W1[:, :, 256:512], wv[:, :, 1152:1408], [], ["W1"], "W1a", q="pool")
        self.dma(W2[:, :, 0:32], wv[:, :, 1408:1440], [], ["W2"], "W2a", q="pool")
        self.dma(W2[:, :, 32:416], wv[:, :, 768:1152], [], ["W2"], "W2a", q="pool")
        self.dma(W3, wv[:, :, 0:512], [], ["W3"], "W3a", q="pool")
        self.wload(Wuq, self.w_uq, "Wuq")
        ukv = self.w_ukv.rearrange("(k p) (h two d) -> p k h two d", p=128, two=2, d=64)
        for k in range(2):
            self.dma(Wuk[:, k, :].rearrange("p (h d) -> p h d", d=64), ukv[:, k, :, 0, :], [], ["Wuk"], "Wuk", q="pool")
            self.dma(Wuv[:, k, :].rearrange("p (h d) -> p h d", d=64), ukv[:, k, :, 1, :], [], ["Wuv"], "Wuv", q="pool")
        self.bcast_load(g0, self.norm_mix[0, :], "g0")
        self.bcast_load(gq, self.a_q_norm, "gq")
        self.bcast_load(gk, self.a_k_norm, "gk")
        self.bcast_load(gbq, self.b_q_norm, "gbq")
        self.bcast_load(gbkv, self.b_kv_norm, "gbkv")
        xt = [T([1024], F32) for _ in range(2)]
        tab = [T([192], F32) for _ in range(2)]
        hn = [T([1024], BF16) for _ in range(2)]
        hnT = [T([8, 128], BF16) for _ in range(2)]
        t1 = T([512], F32)
        t2 = T([512], F32)
        sqt = T([512], F32)
        ssq = T([16], F32)
        rs = T([16], F32)
        nrm = T([512], F32)
        kbf = T([128], BF16)
        qbf = T([512], BF16)
        ckvn = T([256], BF16)
        cqn = T([384], BF16)
        krbf = T([128], BF16)
        cqT = T([3, 128], BF16)
        qB = T([8, 128], BF16)
        P.add("pool", lambda e: e.memset(krbf, 0.0), [], ["krbf"])
        P.add("pool", lambda e: e.memset(qB, 0.0), [], ["qB"])
        ckvT = [T([2, 512], BF16) for _ in range(2)]
        kA_st = [T([512], BF16) for _ in range(2)]
        kB_st = [T([8, 512], BF16, parts=64) for _ in range(2)]
        kr_st = [T([512], BF16, parts=32) for _ in range(2)]
        v_st = [T([10, 4, 65], BF16) for _ in range(2)]
        qA_st = [T([4, 512], BF16) for _ in range(2)]
        qB_st = [T([8, 512], BF16, parts=96) for _ in range(2)]
        for gp in range(2):
            vs = v_st[gp]
            P.add("pool", lambda e, vs=vs: e.memset(vs, 1.0), [], ["v_st%d" % gp])
        psb = [ps[:, b, :].bitcast(BF16) for b in range(8)]

        groups = [(t, 4) for t in range(0, 32, 4)] + [(32, 2)] + [(t, 4) for t in range(34, 62, 4)] + [(62, 2)]
        import os
        KSTOP = int(os.environ.get('K_STOP', '99'))
        groups = groups[:int(os.environ.get('K_MAXG', '99'))]
        for gi, (tg, ng) in enumerate(groups):
            gp = gi % 2
            own_g = tg < NT_OWN
            for j in range(min(ng, int(os.environ.get('K_NG', '9')))):
                t = tg + j
                par = t % 2
                own = t < NT_OWN
                xk, tk, hk, hTk = "xt%d" % par, "tab%d" % par, "hn%d" % par, "hnT%d" % par
                self.dma(xt[par], self.x_d[t * 128:(t + 1) * 128, :], [], [xk], xk)
                self.dma(tab[par], self.tab_d[t * 128:(t + 1) * 128, :], [], [tk], tk)
                self.rms_to_T(xt[par], xk, g0, "g0", hn[par], hk, hnT[par], hTk, psb[0], "ps0", par)
                if KSTOP <= 1:
                    continue
                for k in range(8):
                    self.mm(ps[:, 1, :], hnT[par][:, k, :], W1[:, k, :], k == 0, k == 7, [hTk, "W1"], ["ps1"])
                n2 = 416 if own else 32
                for k in range(8):
                    self.mm(ps[:, 2, 0:n2], hnT[par][:, k, :], W2[:, k, 0:n2], k == 0, k == 7, [hTk, "W2"], ["ps2"])
                if own:
                    for k in range(8):
                        self.mm(ps[:, 3, :], hnT[par][:, k, :], W3[:, k, :], k == 0, k == 7, [hTk, "W3"], ["ps3"])
                if KSTOP <= 2:
                    continue
                z1, z2, z3 = ps[:, 1, :], ps[:, 2, :], ps[:, 3, :]
                CA, SA, CB, SB = tab[par][:, 0:64], tab[par][:, 64:128], tab[par][:, 128:160], tab[par][:, 160:192]
                self.copy("act", v_st[gp][:, 0:2, j, 0:64], z1[:, 128:256].rearrange("p (h d) -> p h d", d=64),
                          ["ps1"], ["v_st%d" % gp])
                if KSTOP <= 3:
                    continue
                self.act(sqt[:, 0:128], z1[:, 0:128], AF.Square, ["ps1"], ["sqt"])
                P.add("dve", lambda e: e.tensor_reduce(out=ssq[:, 0:2], in_=sqt[:, 0:128].rearrange("p (h d) -> p h d", d=64),
                                                       axis=AX.X, op=ALU.add), ["sqt"], ["ssq"])
                self.ts("dve", rs[:, 0:2], ssq[:, 0:2], 1.0 / 64, EPS, ALU.mult, ALU.add, ["ssq"], ["rs"])
                self.act(rs[:, 0:2], rs[:, 0:2], AF.Sqrt, ["rs"], ["rs"])
                P.add("dve", lambda e: e.reciprocal(out=rs[:, 0:2], in_=rs[:, 0:2]), ["rs"], ["rs"])
                n3 = nrm[:, 0:128].rearrange("p (h d) -> p h d", d=64)
                self.tt("dve", n3, z1[:, 0:128].rearrange("p (h d) -> p h d", d=64),
                        rs[:, 0:2].unsqueeze(2).to_broadcast([128, 2, 64]), ALU.mult, ["ps1", "rs"], ["nrm"])
                self.tt("pool", n3, n3, gk.unsqueeze(1).to_broadcast([128, 2, 64]), ALU.mult, ["nrm", "gk"], ["nrm"])
                self.rope(kbf.rearrange("p (h d) -> p h d", d=64), n3, CA, SA, 2, 64,
                          t1[:, 0:128].rearrange("p (h d) -> p h d", d=64),
                          t2[:, 0:128].rearrange("p (h d) -> p h d", d=64), [tk], "nrm", "kbf")
                if KSTOP <= 4:
                    continue
                self.act(sqt[:, 0:256], z1[:, 256:512], AF.Square, ["ps1"], ["sqt", "ssq"], accum_out=ssq[:, 2:3])
                self.ts("dve", rs[:, 2:3], ssq[:, 2:3], 1.0 / 256, EPS, ALU.mult, ALU.add, ["ssq"], ["rs"])
                self.act(rs[:, 2:3], rs[:, 2:3], AF.Sqrt, ["rs"], ["rs"])
                P.add("dve", lambda e: e.reciprocal(out=rs[:, 2:3], in_=rs[:, 2:3]), ["rs"], ["rs"])
                self.stt("dve", ckvn, z1[:, 256:512], rs[:, 2:3], gbkv, ALU.mult, ALU.mult, ["ps1", "rs", "gbkv"], ["ckvn"])
                if KSTOP <= 5:
                    continue
                self.copy("act", nrm[:, 0:32], z2[:, 0:32], ["ps2"], ["nrm"])
                self.rope(krbf[:, 0:32].unsqueeze(1), nrm[:, 0:32].unsqueeze(1), CB, SB, 1, 32,
                          t1[:, 0:32].unsqueeze(1), t2[:, 0:32].unsqueeze(1), [tk], "nrm", "krbf", wide=(krbf, t1[:, 0:128]))
                if os.environ.get('K_EXP2'):
                    self.copy('dve', ssq[:, 12:13], rs[:, 12:13], [], ['dummy'])
                if KSTOP <= 6:
                    continue
                self.tr(psb[4][:, 0:128], kbf, ["kbf"], ["ps4"])
                self.tr(psb[4][:, 128:256], ckvn[:, 0:128], ["ckvn"], ["ps4"])
                self.tr(psb[4][:, 256:384], ckvn[:, 128:256], ["ckvn"], ["ps4"])
                self.tr(psb[6][:, 0:128], krbf, ["krbf"], ["ps6"])
                self.copy("act", kA_st[gp][:, j * 128:(j + 1) * 128], psb[4][:, 0:128], ["ps4"], ["kA_st%d" % gp])
                self.copy("dve", ckvT[gp][:, :, j * 128:(j + 1) * 128],
                          psb[4][:, 128:384].rearrange("p (c t) -> p c t", c=2), ["ps4"], ["ckvT%d" % gp])
                self.copy("act", kr_st[gp][:, j * 128:(j + 1) * 128], psb[6][0:32, 0:128], ["ps6"], ["kr_st%d" % gp])
                if KSTOP <= 7:
                    continue
                for k in range(2):
                    self.mm(ps[:, 5, :], ckvT[gp][:, k, j * 128:(j + 1) * 128], Wuv[:, k, :], k == 0, k == 1,
                            ["ckvT%d" % gp, "Wuv"], ["ps5"])
                self.copy("dve", v_st[gp][:, 2:10, j, 0:64], ps[:, 5, :].rearrange("p (h d) -> p h d", d=64),
                          ["ps5"], ["v_st%d" % gp])
                if KSTOP <= 8:
                    continue
                if own:
                    self.act(sqt, z3, AF.Square, ["ps3"], ["sqt"])
                    P.add("dve", lambda e: e.tensor_reduce(out=ssq[:, 4:12], in_=sqt.rearrange("p (h d) -> p h d", d=64),
                                                           axis=AX.X, op=ALU.add), ["sqt"], ["ssq"])
                    self.ts("dve", rs[:, 4:12], ssq[:, 4:12], 1.0 / 64, EPS, ALU.mult, ALU.add, ["ssq"], ["rs"])
                    self.act(rs[:, 4:12], rs[:, 4:12], AF.Sqrt, ["rs"], ["rs"])
                    P.add("dve", lambda e: e.reciprocal(out=rs[:, 4:12], in_=rs[:, 4:12]), ["rs"], ["rs"])
                    n8 = nrm.rearrange("p (h d) -> p h d", d=64)
                    self.tt("dve", n8, z3.rearrange("p (h d) -> p h d", d=64),
                            rs[:, 4:12].unsqueeze(2).to_broadcast([128, 8, 64]), ALU.mult, ["ps3", "rs"], ["nrm"])
                    self.tt("pool", n8, n8, gq.unsqueeze(1).to_broadcast([128, 8, 64]), ALU.mult, ["nrm", "gq"], ["nrm"])
                    self.rope(qbf.rearrange("p (h d) -> p h d", d=64), n8, CA, SA, 8, 64,
                              t1.rearrange("p (h d) -> p h d", d=64), t2.rearrange("p (h d) -> p h d", d=64),
                              [tk], "nrm", "qbf")
                    for c in range(4):
                        self.tr(psb[6][:, c * 128:(c + 1) * 128], qbf[:, c * 128:(c + 1) * 128], ["qbf"], ["ps6"])
                    self.copy("act", qA_st[gp][:, :, j * 128:(j + 1) * 128],
                              psb[6][:, 0:512].rearrange("p (c t) -> p c t", c=4), ["ps6"], ["qA_st%d" % gp])
                    self.act(sqt[:, 0:384], z2[:, 32:416], AF.Square, ["ps2"], ["sqt", "ssq"], accum_out=ssq[:, 3:4])
                    self.ts("dve", rs[:, 3:4], ssq[:, 3:4], 1.0 / 384, EPS, ALU.mult, ALU.add, ["ssq"], ["rs"])
                    self.act(rs[:, 3:4], rs[:, 3:4], AF.Sqrt, ["rs"], ["rs"])
                    P.add("dve", lambda e: e.reciprocal(out=rs[:, 3:4], in_=rs[:, 3:4]), ["rs"], ["rs"])
                    self.stt("dve", cqn, z2[:, 32:416], rs[:, 3:4], gbq, ALU.mult, ALU.mult, ["ps2", "rs", "gbq"], ["cqn"])
                    for c in range(3):
                        self.tr(psb[6][:, c * 128:(c + 1) * 128], cqn[:, c * 128:(c + 1) * 128], ["cqn"], ["ps6"])
                    self.copy("dve", cqT, psb[6][:, 0:384].rearrange("p (c t) -> p c t", c=3), ["ps6"], ["cqT"])
                    for k in range(3):
                        self.mm(ps[:, 7, :], cqT[:, k, :], Wuq[:, k, 0:512], k == 0, k == 2, ["cqT", "Wuq"], ["ps7"])
                    for k in range(3):
                        self.mm(ps[:, 3, 0:256], cqT[:, k, :], Wuq[:, k, 512:768], k == 0, k == 2, ["cqT", "Wuq"], ["ps3"])
                    self.copy("act", nrm, ps[:, 7, :], ["ps7"], ["nrm"])
                    self.copy("act", sqt[:, 0:256], ps[:, 3, 0:256], ["ps3"], ["sqt"])
                    qfull = [None] * 8
                    for hb in range(8):
                        lo = hb * 96
                        if lo + 64 <= 512:
                            self.copy("pool", qB[:, hb, 0:64], nrm[:, lo:lo + 64], ["nrm"], ["qB"])
                        elif lo >= 512:
                            self.copy("pool", qB[:, hb, 0:64], sqt[:, lo - 512:lo - 512 + 64], ["sqt"], ["qB"])
                        else:
                            a = 512 - lo
                            self.copy("pool", qB[:, hb, 0:a], nrm[:, lo:512], ["nrm"], ["qB"])
                            self.copy("pool", qB[:, hb, a:64], sqt[:, 0:64 - a], ["sqt"], ["qB"])
                        r0 = lo + 64
                        if r0 + 32 <= 512:
                            src, sk = nrm[:, r0:r0 + 32], "nrm"
                        else:
                            src, sk = sqt[:, r0 - 512:r0 - 512 + 32], "sqt"
                        self.rope(qB[:, hb:hb + 1, 64:96], src.unsqueeze(1), CB, SB, 1, 32,
                                  t1[:, 0:32].unsqueeze(1), t2[:, 0:32].unsqueeze(1), [tk], sk, "qB")
                    for hb in range(8):
                        self.tr(psb[0][:, hb * 128:(hb + 1) * 128], qB[:, hb, :], ["qB"], ["ps0"])
                    self.copy("act", qB_st[gp][:, :, j * 128:(j + 1) * 128],
                              psb[0][0:96, :].rearrange("p (h t) -> p h t", h=8), ["ps0"], ["qB_st%d" % gp])
            ntok = ng * 128
            t0 = tg * 128
            for hb in range(8):
                bank = 1 + (hb % 3)
                for k in range(2):
                    self.mm(ps[0:64, bank, 0:ntok], Wuk[:, k, hb * 64:(hb + 1) * 64], ckvT[gp][:, k, 0:ntok],
                            k == 0, k == 1, ["ckvT%d" % gp, "Wuk"], ["ps%d" % bank])
                self.copy("act" if hb % 2 else "dve", kB_st[gp][:, hb, 0:ntok], ps[0:64, bank, 0:ntok],
                          ["ps%d" % bank], ["kB_st%d" % gp])
            sk = "stA%d" % gp
            self.dma(self.kT_d[0, :, t0:t0 + ntok], kA_st[gp][0:64, 0:ntok], ["kA_st%d" % gp], [], sk)
            self.dma(self.kT_d[1, :, t0:t0 + ntok], kA_st[gp][64:128, 0:ntok], ["kA_st%d" % gp], [], sk)
            self.dma(self.kT_d[2:10, :, t0:t0 + ntok].rearrange("h d t -> d h t"), kB_st[gp][:, :, 0:ntok],
                     ["kB_st%d" % gp], [], sk)
            self.dma(self.krT_d[:, t0:t0 + ntok], kr_st[gp][:, 0:ntok], ["kr_st%d" % gp], [], sk)
            self.dma(self.vE_d[:, :, tg:tg + ng, :].rearrange("i p t e -> p i t e"), v_st[gp][:, :, 0:ng, :],
                     ["v_st%d" % gp], [], sk)
            if own_g:
                qv = self.qT_d[0:8, 0:64, t0:t0 + ntok].rearrange("(c two) d t -> two d c t", two=2)
                for two in range(2):
                    self.dma(qv[two], qA_st[gp][two * 64:(two + 1) * 64, :, 0:ntok], ["qA_st%d" % gp], [], sk)
                self.dma(self.qT_d[8:16, :, t0:t0 + ntok].rearrange("h d t -> d h t"), qB_st[gp][:, :, 0:ntok],
                         ["qB_st%d" % gp], [], sk)

    def phase_B(self):
        import os
        P, ps, T = self.P, self.ps, self.T
        self.reset_arena()
        kbuf = [T([S_ALL], BF16, parts=96) for _ in range(2)]
        vbuf = [T([NT_ALL, 65], BF16) for _ in range(2)]
        qbuf = [T([TOK_OWN], BF16, parts=96) for _ in range(2)]
        pT = [T([3, 512], BF16) for _ in range(2)]
        rec = T([512], F32)
        osb = T([512], F32, parts=64)
        mix = [T([512], BF16, parts=64) for _ in range(2)]
        ones = self.ones
        for par in range(2):
            self.dma(kbuf[par][64:96, :], self.krT_d[:, :], [], ["kb%d" % par], "kr%d" % par)
        heads = [int(x) for x in os.environ.get("K_HEADS", ",".join(str(i) for i in range(16))).split(",")]
        ngr = int(os.environ.get("K_BG", "9"))
        for hi, h in enumerate(heads):
            par = hi % 2
            idx = h // 4 if h < 8 else h - 6
            dk = 64 if h < 8 else 96
            scale = float(dk) ** -0.5
            kk, vk, qk = "kb%d" % par, "vb%d" % par, "qb%d" % par
            self.dma(kbuf[par][0:64, :], self.kT_d[idx, :, :], [], [kk], "k%d" % par)
            self.dma(vbuf[par], self.vE_d[idx, :, :, :], [], [vk], "v%d" % par)
            self.dma(qbuf[par][0:dk, :], self.qT_d[h, 0:dk, :], [], [qk], "q%d" % par)
            for g in range(ngr):
                q0 = g * 512
                nq = 512 if g < 8 else 256
                kt = 0
                bi = 0
                while kt < NT_ALL:
                    nb = min(3, NT_ALL - kt)
                    sb = bi % 2
                    for i in range(nb):
                        self.mm(ps[:, sb * 3 + i, 0:nq], kbuf[par][0:dk, (kt + i) * 128:(kt + i + 1) * 128],
                                qbuf[par][0:dk, q0:q0 + nq], True, True, [kk, qk], ["S%d" % sb])
                    self.act(pT[sb][:, 0:nb, 0:nq], ps[:, sb * 3:sb * 3 + nb, 0:nq], AF.Exp,
                             ["S%d" % sb], ["pT%d" % sb], scale=scale)
                    for i in range(nb):
                        self.mm(ps[0:65, 6, 0:nq], vbuf[par][:, kt + i, :], pT[sb][:, i, 0:nq],
                                kt + i == 0, kt + i == NT_ALL - 1, [vk, "pT%d" % sb], ["O"])
                    kt += nb
                    bi += 1
                mk = "mix%d" % (g % 2)
                P.add("dve", lambda e, nq=nq: e.reciprocal(out=rec[64:65, 0:nq], in_=ps[64:65, 6, 0:nq]), ["O"], ["rec"])
                self.mm(ps[0:64, 7, 0:nq], ones[64:65, 0:64], rec[64:65, 0:nq], True, True, ["rec", "ones"], ["bc"])
                self.copy("act", osb[:, 0:nq], ps[0:64, 6, 0:nq], ["O"], ["osb"])
                self.tt("dve", mix[g % 2][:, 0:nq], osb[:, 0:nq], ps[0:64, 7, 0:nq], ALU.mult, ["osb", "bc"], [mk])
                self.dma(self.mix0_d[h, :, q0:q0 + nq], mix[g % 2][:, 0:nq], [mk], [], "mo%d" % (g % 2))

    def phase_ffn(self, layer):
        import os
        P, ps, T = self.P, self.ps, self.T
        self.reset_arena()
        L = layer
        wup = T([8, 4096], BF16)
        wdn = T([32, 1024], BF16)
        wo = T([8, 1024], BF16)
        gf = T([1024], F32)
        self.wload(wo, self.w_out0 if L == 0 else self.w_out1, "wo")
        upv = self.w_up[L].rearrange("(k p) n -> p k n", p=128)
        for k in range(8):
            self.dma(wup[:, k, :], upv[:, k, :], [], ["wup"], "wup", q="pool")
        dnv = self.w_down[L].rearrange("(k p) n -> p k n", p=128)
        for k4 in range(8):
            self.dma(wdn[:, k4 * 4:(k4 + 1) * 4, :], dnv[:, k4 * 4:(k4 + 1) * 4, :], [], ["wdn"], "wdn", q="pool")
        self.bcast_load(gf, self.norm_ffn[L, :], "gf")
        if L == 1:
            gfin = T([1024], F32)
            self.bcast_load(gfin, self.final_norm, "gfin")
        mixg = T([8, 256], BF16)
        xt = [T([1024], F32) for _ in range(2)]
        h1 = T([2, 1024], F32)
        hn = T([1024], BF16)
        hnT = T([8, 256], BF16)
        uT = T([32, 256], BF16)
        rl = [T([256], F32) for _ in range(2)]
        psb0 = ps[:, 0, :].bitcast(BF16)
        mix_d = self.mix0_d if L == 0 else self.mix1_d
        res_d = self.x_d if L == 0 else self.h1_d
        ngroups = (17 if L == 0 else 16)
        ngroups = min(ngroups, int(os.environ.get("K_FG", "99")))
        junk = self.junk
        for g in range(ngroups):
            t0 = g * 256
            mv = mix_d[:, :, t0:t0 + 256].rearrange("(c two) d t -> (two d) c t", two=2)
            self.dma(mixg, mv, [], ["mixg"], "mixg")
            for j in range(2):
                tok = t0 + j * 128
                hk = "h1_%d" % j
                self.dma(xt[j], res_d[tok:tok + 128, :], [], ["xt%d" % j], "xt%d" % j)
                for half in range(2):
                    hs = slice(half * 512, (half + 1) * 512)
                    for c in range(8):
                        self.mm(ps[:, 1 + half, :], mixg[:, c, j * 128:(j + 1) * 128], wo[:, c, hs],
                                c == 0, c == 7, ["mixg", "wo"], ["ps%d" % (1 + half)])
                    self.tt("dve", h1[:, j, hs], ps[:, 1 + half, :], xt[j][:, hs], ALU.add,
                            ["ps%d" % (1 + half), "xt%d" % j], [hk])
                self.rms_to_T(h1[:, j, :], hk, gf, "gf", hn, "hn", hnT[:, :, j * 128:(j + 1) * 128], "hnT",
                              psb0, "ps0", j)
            for f in range(32):
                bank = 3 + (f % 2)
                rk = "rl%d" % (f % 2)
                for k in range(8):
                    self.mm(ps[:, bank, 0:256], wup[:, k, f * 128:(f + 1) * 128], hnT[:, k, :], k == 0, k == 7,
                            ["wup", "hnT"], ["ps%d" % bank])
                self.act(rl[f % 2], ps[:, bank, 0:256], AF.Relu, ["ps%d" % bank], [rk])
                self.tt("dve" if f % 2 == 0 else "pool", uT[:, f, :], rl[f % 2], rl[f % 2], ALU.mult,
                        [rk], ["uT%d" % (f % 2)])
            for j in range(2):
                tok = t0 + j * 128
                hk = "h1_%d" % j
                for half in range(2):
                    hs = slice(half * 512, (half + 1) * 512)
                    bank = 5 + half
                    for f in range(32):
                        self.mm(ps[:, bank, :], uT[:, f, j * 128:(j + 1) * 128], wdn[:, f, hs], f == 0, f == 31,
                                ["uT0", "uT1", "wdn"], ["ps%d" % bank])
                    self.tt("dve", h1[:, j, hs], ps[:, bank, :], h1[:, j, hs], ALU.add, ["ps%d" % bank, hk], [hk])
                if L == 0:
                    self.dma(self.h1_d[tok:tok + 128, :], h1[:, j, :], [hk], [], "h1o%d" % j)
                else:
                    ssq, rs = self.ssq_x[j], self.rs_x[j]
                    sk, rk2 = "ssqx%d" % j, "rsx%d" % j
                    self.act(junk, h1[:, j, :], AF.Square, [hk], ["junk", sk], accum_out=ssq)
                    self.ts("dve", rs, ssq, 1.0 / 1024, EPS, ALU.mult, ALU.add, [sk], [rk2])
                    self.act(rs, rs, AF.Sqrt, [rk2], [rk2])
                    P.add("dve", lambda e, rs=rs: e.reciprocal(out=rs, in_=rs), [rk2], [rk2])
                    self.stt("dve", xt[j], h1[:, j, :], rs, gfin, ALU.mult, ALU.mult, [hk, rk2, "gfin"], ["xt%d" % j])
                    self.dma(self.out_d[tok:tok + 128, :], xt[j], ["xt%d" % j], [], "outo%d" % j)

    def phase_D(self):
        P, ps, T = self.P, self.ps, self.T
        self.reset_arena()
        wq = T([8, 3072], BF16)
        wv = self.w_qkv.rearrange("(k p) n -> p k n", p=128)
        for i in range(6):
            self.dma(wq[:, :, i * 512:(i + 1) * 512], wv[:, :, i * 512:(i + 1) * 512], [], ["wq"], "wq", q="pool")
        g1 = T([1024], F32)
        self.bcast_load(g1, self.norm_mix[1, :], "g1")
        xt = [T([1024], F32) for _ in range(2)]
        hn = [T([1024], BF16) for _ in range(2)]
        hnT = T([8, 512], BF16)
        qk_st = T([16, 512], BF16)
        v_st = [T([16, 65], BF16) for _ in range(2)]
        for i in range(2):
            vs = v_st[i]
            P.add("pool", lambda e, vs=vs: e.memset(vs, 1.0), [], ["v1st%d" % i])
        psb0 = ps[:, 0, :].bitcast(BF16)
        groups = [(t, 4) for t in range(0, 32, 4)] + [(32, 2)]
        for tg, ng in groups:
            ntok = ng * 128
            t0 = tg * 128
            for j in range(ng):
                t = tg + j
                par = t % 2
                self.dma(xt[par], self.h1_d[t * 128:(t + 1) * 128, :], [], ["xt%d" % par], "xt%d" % par)
                self.rms_to_T(xt[par], "xt%d" % par, g1, "g1", hn[par], "hn%d" % par,
                              hnT[:, :, j * 128:(j + 1) * 128], "hnT", psb0, "ps0", par)
            for cc in range(16):
                bank = 1 + (cc % 2)
                col = cc * 128
                for k in range(8):
                    self.mm(ps[:, bank, 0:ntok], wq[:, k, col:col + 128], hnT[:, k, 0:ntok], k == 0, k == 7,
                            ["wq", "hnT"], ["ps%d" % bank])
                self.copy("act" if cc % 2 else "dve", qk_st[:, cc, 0:ntok], ps[:, bank, 0:ntok],
                          ["ps%d" % bank], ["qk_st"])
            self.dma(self.q1T_d[:, :, t0:t0 + ntok].rearrange("c p t -> p c t"), qk_st[:, 0:8, 0:ntok],
                     ["qk_st"], [], "qko")
            self.dma(self.k1T_d[:, :, t0:t0 + ntok].rearrange("c p t -> p c t"), qk_st[:, 8:16, 0:ntok],
                     ["qk_st"], [], "qko")
            for j in range(ng):
                t = tg + j
                par = t % 2
                for half in range(2):
                    bank = 3 + half
                    for k in range(8):
                        self.mm(ps[:, bank, :], hnT[:, k, j * 128:(j + 1) * 128],
                                wq[:, k, 2048 + half * 512:2048 + (half + 1) * 512], k == 0, k == 7,
                                ["wq", "hnT"], ["ps%d" % bank])
                    self.copy("act" if half else "dve", v_st[par][:, half * 8:(half + 1) * 8, 0:64],
                              ps[:, bank, :].rearrange("p (h d) -> p h d", d=64), ["ps%d" % bank], ["v1st%d" % par])
                self.dma(self.v1_d[:, t, :, :], v_st[par], ["v1st%d" % par], [], "v1o%d" % par)

    def phase_E(self):
        import os
        P, ps, T = self.P, self.ps, self.T
        self.reset_arena()
        k1T = T([8, TOK_OWN], BF16)
        v1f = T([NT_OWN * 16 * 65], BF16)
        v1 = v1f.rearrange("p (t h e) -> p t h e", t=NT_OWN, h=16, e=65)
        bias = T([16, 640], F32)
        qt = [T([8, 128], BF16) for _ in range(2)]
        s_sb = T([2, 640], F32)
        pT = T([2, 640], BF16)
        rec = T([256], F32)
        osb = T([256], F32, parts=64)
        mst = [T([16, 128], BF16, parts=64) for _ in range(2)]
        ones = self.ones
        self.dma(k1T, self.k1T_d.rearrange("c p t -> p c t"), [], ["k1T"], "k1T")
        self.dma(v1f, self.v1_d.rearrange("p t h e -> p (t h e)"), [], ["v1"], "v1")
        ntiles = min(NT_OUT, int(os.environ.get("K_ET", "99")))
        for t in range(ntiles):
            par = t % 2
            ws = max(t - 2, 0)
            if t <= 2:
                self.dma(bias, self.biasT[t, :, :, :], [], ["bias"], "bias")
            self.dma(qt[par], self.q1T_d[:, :, t * 128:(t + 1) * 128].rearrange("c p t -> p c t"), [], ["qt%d" % par], "qt%d" % par)
            for c in range(8):
                for two in range(2):
                    h = 2 * c + two
                    pb = two * 64
                    b0 = two * 2
                    for j in range(5):
                        self.mm(ps[:, b0 + j // 4, (j % 4) * 128:(j % 4 + 1) * 128],
                                k1T[pb:pb + 64, c, (ws + j) * 128:(ws + j + 1) * 128], qt[par][pb:pb + 64, c, :],
                                True, True, ["k1T", "qt%d" % par], ["S%d" % two])
                    sv = ps[:, b0:b0 + 2, :].rearrange("p b n -> p (b n)")[:, 0:640]
                    self.stt("dve", s_sb[:, two, :], sv, 0.125, bias[:, h, :], ALU.mult, ALU.add,
                             ["S%d" % two, "bias"], ["s_sb"])
                self.act(pT, s_sb, AF.Exp, ["s_sb"], ["pT"])
                for two in range(2):
                    h = 2 * c + two
                    for j in range(5):
                        self.mm(ps[0:65, 4, two * 128:(two + 1) * 128], v1[:, ws + j, h, :], pT[:, two, j * 128:(j + 1) * 128],
                                j == 0, j == 4, ["v1", "pT"], ["O"])
                P.add("dve", lambda e: e.reciprocal(out=rec[64:65, :], in_=ps[64:65, 4, 0:256]), ["O"], ["rec"])
                self.mm(ps[0:64, 5, 0:256], ones[64:65, 0:64], rec[64:65, :], True, True, ["rec", "ones"], ["bc"])
                self.copy("act", osb, ps[0:64, 4, 0:256], ["O"], ["osb"])
                self.tt("dve", mst[par][:, 2 * c:2 * c + 2, :], osb.rearrange("p (a q) -> p a q", a=2),
                        ps[0:64, 5, 0:256].rearrange("p (a q) -> p a q", a=2), ALU.mult, ["osb", "bc"], ["mst%d" % par])
            self.dma(self.mix1_d[:, :, t * 128:(t + 1) * 128].rearrange("h d t -> d h t"), mst[par],
                     ["mst%d" % par], [], "m1o%d" % par)


def _perm(half):
    rows = np.arange(128) if half == 0 else np.arange(127, -1, -1)
    return (rows[:, None] * 64 + np.arange(64)[None, :]).reshape(-1)


def _tables(perm):
    row = (perm // 64).astype(np.float32)
    col = (perm % 64).astype(np.float32)

    def cs(pos, d):
        inv = (np.float32(10000.0) ** (-np.arange(0, d, 2, dtype=np.float32) / np.float32(d))).astype(np.float32)
        ang = (pos[:, None] * inv[None, :]).astype(np.float32)
        return np.cos(ang).astype(np.float32), np.sin(ang).astype(np.float32)

    out = []
    for d in (32, 16):
        cr, sr = cs(row, d)
        cc, sc = cs(col, d)
        out.append(np.concatenate([cr, cr, cc, cc], axis=1))
        out.append(np.concatenate([-sr, sr, -sc, sc], axis=1))
    return np.ascontiguousarray(np.concatenate(out, axis=1), dtype=np.float32)


def _bias_tables(rpb, half):
    out = np.full((3, 128, 16, 5, 128), NEG, dtype=np.float32)
    qi = np.arange(128)
    for v in range(3):
        t = v
        ws = max(t - 2, 0)
        lrq = 2 * t + qi // 64
        cq = qi % 64
        grq = lrq if half == 0 else 127 - lrq
        rs_ = np.clip(grq - 4, 0, 120)
        cs_ = np.clip(cq - 8, 0, 48)
        for j in range(5):
            kp = np.arange(128)
            lrk = 2 * (ws + j) + kp // 64
            ck = kp % 64
            grk = lrk if half == 0 else 127 - lrk
            valid = ((grk[:, None] >= rs_[None, :]) & (grk[:, None] < rs_[None, :] + 8) &
                     (ck[:, None] >= cs_[None, :]) & (ck[:, None] < cs_[None, :] + 16))
            ridx = np.clip(grk[:, None] - grq[None, :] + 7, 0, 14)
            cidx = np.clip(ck[:, None] - cq[None, :] + 15, 0, 30)
            g = rpb[:, ridx, cidx]
            g = np.where(valid[None], g, np.float32(NEG))
            out[v, :, :, j, :] = g.transpose(1, 0, 2)
    return np.ascontiguousarray(out.reshape(3, 128, 16, 640))


_CACHE = {}


def make_in_maps(x, norm_mix, ev_w_in, ev_a_q_norm, ev_a_k_norm, ev_b_q_norm, ev_b_w_uq,
                 ev_b_kv_norm, ev_b_w_ukv, ev_w_out, od_w_qkv, od_rpb, od_w_out,
                 norm_ffn, ffn_w_up, ffn_w_down, final_norm):
    f = lambda a: np.ascontiguousarray(np.asarray(a), dtype=np.float32)
    shared = {
        "norm_mix": f(norm_mix), "norm_ffn": f(norm_ffn), "final_norm": f(final_norm),
        "w_in": f(ev_w_in)[0], "a_q_norm": f(ev_a_q_norm)[0], "a_k_norm": f(ev_a_k_norm)[0],
        "b_q_norm": f(ev_b_q_norm)[0], "b_kv_norm": f(ev_b_kv_norm)[0], "w_uq": f(ev_b_w_uq)[0],
        "w_ukv": f(ev_b_w_ukv)[0], "w_out0": f(ev_w_out)[0], "w_qkv": f(od_w_qkv)[0],
        "w_out1": f(od_w_out)[0], "w_up": f(ffn_w_up), "w_down": f(ffn_w_down),
    }
    x = f(x)
    rpb = f(od_rpb)[0]
    perms = [_perm(0), _perm(1)]
    tabs = [_tables(p) for p in perms]
    biases = [_bias_tables(rpb, h) for h in range(2)]
    in_maps = []
    for c in range(8):
        b, h = c // 2, c % 2
        m = dict(shared)
        m["x"] = np.ascontiguousarray(x[b][perms[h]])
        m["tab"] = tabs[h]
        m["biasT"] = biases[h]
        in_maps.append(m)
    return in_maps, perms


def kernel(**inputs):
    in_maps, perms = make_in_maps(**inputs)
    if "nc" not in _CACHE:
        _CACHE["nc"] = Builder(debug=False).build()
    res = run_bass_kernel_spmd(_CACHE["nc"], in_maps, core_ids=list(range(8)))
    out = np.empty((4, S_ALL, 1024), dtype=np.float32)
    for c in range(8):
        b, h = c // 2, c % 2
        out[b][perms[h][:TOK_OUT]] = res.results[c]["out"]
    return out
```

```python
import contextlib
import numpy as np
import concourse.bass as bass
import concourse.mybir as mybir
from concourse.bass_utils import run_bass_kernel_spmd

F32 = mybir.dt.float32
BF16 = mybir.dt.bfloat16
AF = mybir.ActivationFunctionType
ALU = mybir.AluOpType
AX = mybir.AxisListType

COMPUTE = ("pe", "act", "dve", "pool")
ALLENG = COMPUTE + ("sp",)

S_ALL = 8192
NT_ALL = 64
NT_OWN = 34
NT_OUT = 32
TOK_OWN = NT_OWN * 128
TOK_OUT = NT_OUT * 128
EPS = 1e-6
NEG = -30000.0


class Op:
    __slots__ = ("id", "eng", "fn", "deps", "dma", "cum", "marked", "cnt")


class Prog:
    def __init__(self, nc, same_eng_sync=True):
        self.nc = nc
        self.same_eng_sync = same_eng_sync
        self.ops = []
        self.last_w = {}
        self.rd_eng = {}
        self.rd_dma = {}
        self.dma_cum = {}
        self.dma_last = {}
        self.last_on_eng = {}

    def add(self, eng, fn, reads=(), writes=(), dma=None, extra_deps=()):
        op = Op()
        op.id = len(self.ops)
        op.eng = eng
        op.fn = fn
        op.dma = dma
        op.marked = dma is not None
        op.cnt = 0
        op.cum = 0
        deps = set(extra_deps)
        for k in reads:
            w = self.last_w.get(k)
            if w is not None:
                deps.add(w)
        for k in writes:
            w = self.last_w.get(k)
            if w is not None:
                deps.add(w)
            for r in self.rd_eng.get(k, {}).values():
                deps.add(r)
            for r in self.rd_dma.get(k, ()):
                deps.add(r)
        d2 = set()
        for d in deps:
            o = self.ops[d]
            if o.dma is not None:
                d2.add(self.dma_last[o.dma])
            else:
                d2.add(d)
        op.deps = d2
        if dma is not None:
            self.dma_cum[dma] = self.dma_cum.get(dma, 0) + 16
            op.cum = self.dma_cum[dma]
            self.dma_last[dma] = op.id
        else:
            self.last_on_eng[eng] = op.id
        for k in writes:
            self.last_w[k] = op.id
            self.rd_eng[k] = {}
            self.rd_dma[k] = []
        for k in reads:
            if k in writes:
                continue
            if dma is not None:
                self.rd_dma.setdefault(k, []).append(op.id)
            else:
                self.rd_eng.setdefault(k, {})[eng] = op.id
        self.ops.append(op)
        return op

    def barrier(self):
        deps = set(self.last_on_eng.values()) | set(self.dma_last.values())
        for e in ALLENG:
            self.add(e, None, extra_deps=deps)
        self.last_w = {}
        self.rd_eng = {}
        self.rd_dma = {}

    def emit(self):
        nc = self.nc
        ops = self.ops
        ses = self.same_eng_sync

        def skip(o, op):
            return (o.dma is None and op.dma is None and o.eng == op.eng
                    and (o.eng == "pe" or not ses) and op.fn is not None)

        for op in ops:
            for d in op.deps:
                o = ops[d]
                if o.dma is not None or skip(o, op):
                    continue
                o.marked = True
        run = {e: 0 for e in ALLENG}
        for op in ops:
            if op.dma is None:
                if op.marked and op.fn is not None:
                    run[op.eng] += 1
                op.cnt = run[op.eng]
        per_eng = {e: [] for e in ALLENG}
        for op in ops:
            per_eng[op.eng].append(op)
        with contextlib.ExitStack() as st:
            sems = {e: st.enter_context(nc.semaphore("s_" + e)) for e in COMPUTE}
            dsems = {k: st.enter_context(nc.semaphore("d_%d" % i))
                     for i, k in enumerate(self.dma_cum)}
            block = st.enter_context(nc.Block())

            def run_engine(eng_name, eng):
                waited = {}
                for op in per_eng[eng_name]:
                    need = {}
                    for d in op.deps:
                        o = ops[d]
                        if o.dma is not None:
                            s = ("d", o.dma)
                            v = o.cum
                        else:
                            if skip(o, op):
                                continue
                            if o.fn is None:
                                continue
                            s = ("c", o.eng)
                            v = o.cnt
                        if v <= 0 or waited.get(s, 0) >= v:
                            continue
                        if need.get(s, 0) < v:
                            need[s] = v
                    for s, v in need.items():
                        sem = dsems[s[1]] if s[0] == "d" else sems[s[1]]
                        eng.wait_ge(sem, v)
                        waited[s] = v
                    if op.fn is None:
                        continue
                    ins = op.fn(eng)
                    if op.dma is not None:
                        ins.then_inc(dsems[op.dma], 16)
                    elif op.marked:
                        ins.then_inc(sems[op.eng], 1)
                for k, sem in dsems.items():
                    last = ops[self.dma_last[k]]
                    if last.eng == eng_name and waited.get(("d", k), 0) < self.dma_cum[k]:
                        eng.wait_ge(sem, self.dma_cum[k])

            @block.tensor
            def _(e):
                run_engine("pe", e)

            @block.scalar
            def _(e):
                run_engine("act", e)

            @block.vector
            def _(e):
                run_engine("dve", e)

            @block.gpsimd
            def _(e):
                run_engine("pool", e)

            @block.sync
            def _(e):
                run_engine("sp", e)


class Builder:
    def __init__(self, debug=False, phases="ABCDEF"):
        self.debug = debug
        self.phases = phases
        self.nc = bass.Bass("TRN2", target_bir_lowering=False)
        import os
        self.P = Prog(self.nc, same_eng_sync=not os.environ.get("K_NOSES"))
        self.uid = 0

    def reset_arena(self):
        self.off = self.base_off

    def T(self, shape, dt, parts=128):
        n = int(np.prod(shape))
        nbytes = n * (4 if dt == F32 else 2)
        nbytes = (nbytes + 63) // 64 * 64
        assert self.off + nbytes <= self.arena_bytes, ("SBUF arena overflow", self.off, nbytes)
        ap = self.arena[0:parts, self.off // 2:(self.off + nbytes) // 2]
        self.off += nbytes
        if dt == F32:
            ap = ap.bitcast(F32)
        ap = ap[:, 0:n]
        if len(shape) == 2:
            ap = ap.rearrange("p (a b) -> p a b", a=shape[0], b=shape[1])
        elif len(shape) == 3:
            ap = ap.rearrange("p (a b c) -> p a b c", a=shape[0], b=shape[1], c=shape[2])
        return ap

    def key(self, s):
        self.uid += 1
        return "%s#%d" % (s, self.uid)

    def dram(self, name, shape, dt, kind=None):
        if kind is None:
            kind = "ExternalOutput" if self.debug else "Internal"
        return self.nc.dram_tensor(name, list(shape), dt, kind=kind).ap()

    def mm(self, out, lhsT, rhs, start, stop, reads, writes):
        self.P.add("pe", lambda e: e.matmul(out, lhsT=lhsT, rhs=rhs, start=start, stop=stop),
                   reads, writes)

    def tr(self, out, in_, reads, writes):
        ident = self.ident
        npart = in_.shape[0]
        self.P.add("pe", lambda e: e.transpose(out=out, in_=in_, identity=ident[0:npart, 0:npart]),
                   list(reads) + ["ident"], writes)

    def act(self, out, in_, func, reads, writes, scale=1.0, accum_out=None):
        if accum_out is None:
            self.P.add("act", lambda e: e.activation(out=out, in_=in_, func=func, scale=scale),
                       reads, writes)
        else:
            self.P.add("act", lambda e: e.activation(out=out, in_=in_, func=func, scale=scale,
                                                     accum_out=accum_out), reads, writes)

    def tt(self, eng, out, in0, in1, op, reads, writes):
        self.P.add(eng, lambda e: e.tensor_tensor(out=out, in0=in0, in1=in1, op=op), reads, writes)

    def ts(self, eng, out, in0, s1, s2, op0, op1, reads, writes):
        if op1 is None:
            self.P.add(eng, lambda e: e.tensor_scalar(out=out, in0=in0, scalar1=s1, scalar2=None,
                                                      op0=op0), reads, writes)
        else:
            self.P.add(eng, lambda e: e.tensor_scalar(out=out, in0=in0, scalar1=s1, scalar2=s2,
                                                      op0=op0, op1=op1), reads, writes)

    def stt(self, eng, out, in0, scalar, in1, op0, op1, reads, writes):
        self.P.add(eng, lambda e: e.scalar_tensor_tensor(out=out, in0=in0, scalar=scalar, in1=in1,
                                                         op0=op0, op1=op1), reads, writes)

    def copy(self, eng, out, in_, reads, writes):
        if eng == "act":
            self.P.add("act", lambda e: e.copy(out=out, in_=in_), reads, writes)
        else:
            self.P.add(eng, lambda e: e.tensor_copy(out=out, in_=in_), reads, writes)

    def dma(self, out, in_, reads, writes, sem, q="sp"):
        self.P.add(q, lambda e: e.dma_start(out=out, in_=in_), reads, writes, dma=sem)

    def rstd(self, out, ssq, n, key):
        self.ts("dve", out, ssq, 1.0 / n, EPS, ALU.mult, ALU.add, [key], [key])
        self.act(out, out, AF.Sqrt, [key], [key])
        self.P.add("dve", lambda e: e.reciprocal(out=out, in_=out), [key], [key])

    def bcast_load(self, dst, src_1d, key, q="sp"):
        n = dst.shape[1]
        self.dma(dst, src_1d.partition_broadcast(128), [], [key], key, q=q)

    def wload(self, dst, src2d, key):
        self.dma(dst, src2d.rearrange("(k p) n -> p k n", p=128), [], [key], key, q="pool")

    def rms_to_T(self, xt, xkey, g_bc, gkey, hn, hnkey, hnT_dst, hnTkey, psb, pskey, par):
        junk, ssq, rs = self.junk, self.ssq_x[par], self.rs_x[par]
        self.act(junk, xt, AF.Square, [xkey], ["junk", "ssqx%d" % par], accum_out=ssq)
        k = "ssqx%d" % par
        self.ts("dve", rs, ssq, 1.0 / 1024, EPS, ALU.mult, ALU.add, [k], ["rsx%d" % par])
        self.act(rs, rs, AF.Sqrt, ["rsx%d" % par], ["rsx%d" % par])
        self.P.add("dve", lambda e: e.reciprocal(out=rs, in_=rs), ["rsx%d" % par], ["rsx%d" % par])
        self.stt("dve", hn, xt, rs, g_bc, ALU.mult, ALU.mult, [xkey, "rsx%d" % par, gkey], [hnkey])
        for c in range(8):
            self.tr(psb[:, c * 128:(c + 1) * 128], hn[:, c * 128:(c + 1) * 128], [hnkey], [pskey])
        self.copy("act", hnT_dst, psb[:, 0:1024].rearrange("p (c t) -> p c t", c=8), [pskey], [hnTkey])

    def rope(self, dst_bf, src, C, S, nh, d, tmp1, tmp2, rk, skey, dkey, wide=None):
        q = d // 4
        e0 = "dve" if skey.startswith("ps") else "pool"
        S5 = S.rearrange("p (b f e) -> p b f e", b=2, f=2, e=q)
        if nh == 1:
            src, dst_bf, tmp1, tmp2 = src[:, 0, :], dst_bf[:, 0, :], tmp1[:, 0, :], tmp2[:, 0, :]
            self.tt(e0, tmp1, src, C, ALU.mult, rk + [skey], ["rt1"])
            s5 = src.rearrange("p (b f e) -> p b f e", b=2, f=2, e=q)
            t5 = tmp2.rearrange("p (b f e) -> p b f e", b=2, f=2, e=q)
            for f in range(2):
                self.tt("dve", t5[:, :, f, :], s5[:, :, 1 - f, :], S5[:, :, f, :], ALU.mult, rk + [skey], ["rt2"])
            self.tt("dve", tmp1, tmp1, tmp2, ALU.add, ["rt1", "rt2"], ["rt1"])
            if wide is not None:
                self.copy("dve", wide[0], wide[1], ["rt1"], [dkey])
            else:
                self.copy("dve", dst_bf, tmp1, ["rt1"], [dkey])
            return
        Cb = C.unsqueeze(1).to_broadcast([128, nh, d])
        self.tt(e0, tmp1, src, Cb, ALU.mult, rk + [skey], ["rt1"])
        s5 = src.rearrange("p h (b f e) -> p h b f e", b=2, f=2, e=q)
        t5 = tmp2.rearrange("p h (b f e) -> p h b f e", b=2, f=2, e=q)
        for b in range(2):
            for f in range(2):
                Sb = S5[:, b, f, :].unsqueeze(1).to_broadcast([128, nh, q])
                self.tt("dve", t5[:, :, b, f, :], s5[:, :, b, 1 - f, :], Sb, ALU.mult,
                        rk + [skey], ["rt2"])
        self.tt("dve", dst_bf, tmp1, tmp2, ALU.add, ["rt1", "rt2"], [dkey])

    def build(self):
        nc = self.nc
        P = self.P
        dbg = self.debug
        EI = "ExternalInput"
        self.x_d = nc.dram_tensor("x", [S_ALL, 1024], F32, kind=EI).ap()
        self.tab_d = nc.dram_tensor("tab", [S_ALL, 192], F32, kind=EI).ap()
        self.norm_mix = nc.dram_tensor("norm_mix", [2, 1024], F32, kind=EI).ap()
        self.norm_ffn = nc.dram_tensor("norm_ffn", [2, 1024], F32, kind=EI).ap()
        self.final_norm = nc.dram_tensor("final_norm", [1024], F32, kind=EI).ap()
        self.w_in = nc.dram_tensor("w_in", [1024, 1440], F32, kind=EI).ap()
        self.a_q_norm = nc.dram_tensor("a_q_norm", [64], F32, kind=EI).ap()
        self.a_k_norm = nc.dram_tensor("a_k_norm", [64], F32, kind=EI).ap()
        self.b_q_norm = nc.dram_tensor("b_q_norm", [384], F32, kind=EI).ap()
        self.b_kv_norm = nc.dram_tensor("b_kv_norm", [256], F32, kind=EI).ap()
        self.w_uq = nc.dram_tensor("w_uq", [384, 768], F32, kind=EI).ap()
        self.w_ukv = nc.dram_tensor("w_ukv", [256, 1024], F32, kind=EI).ap()
        self.w_out0 = nc.dram_tensor("w_out0", [1024, 1024], F32, kind=EI).ap()
        self.w_qkv = nc.dram_tensor("w_qkv", [1024, 3072], F32, kind=EI).ap()
        self.biasT = nc.dram_tensor("biasT", [3, 128, 16, 640], F32, kind=EI).ap()
        self.w_out1 = nc.dram_tensor("w_out1", [1024, 1024], F32, kind=EI).ap()
        self.w_up = nc.dram_tensor("w_up", [2, 1024, 4096], F32, kind=EI).ap()
        self.w_down = nc.dram_tensor("w_down", [2, 4096, 1024], F32, kind=EI).ap()
        self.out_d = nc.dram_tensor("out", [TOK_OUT, 1024], F32, kind="ExternalOutput").ap()
        self.kT_d = self.dram("kT", [10, 64, S_ALL], BF16)
        self.krT_d = self.dram("krT", [32, S_ALL], BF16)
        self.vE_d = self.dram("vE", [10, 128, NT_ALL, 65], BF16)
        self.qT_d = self.dram("qT", [16, 96, TOK_OWN], BF16)
        self.mix0_d = self.dram("mix0", [16, 64, TOK_OWN], BF16)
        self.h1_d = self.dram("h1", [TOK_OWN, 1024], F32)
        self.q1T_d = self.dram("q1T", [8, 128, TOK_OWN], BF16)
        self.k1T_d = self.dram("k1T", [8, 128, TOK_OWN], BF16)
        self.v1_d = self.dram("v1", [128, NT_OWN, 16, 65], BF16)
        self.mix1_d = self.dram("mix1", [16, 64, TOK_OWN], BF16)

        with contextlib.ExitStack() as st:
            self.arena_bytes = 206 * 1024
            self.arena = st.enter_context(nc.sbuf_tensor("arena", [128, self.arena_bytes // 2], BF16))
            self.ps = st.enter_context(nc.psum_tensor("ps", [128, 8, 512], F32))
            st.enter_context(nc.allow_low_precision("bf16 matmul operands, fp32 PSUM accumulation"))
            self.off = 0
            self.ident = self.T([128], BF16)
            self.identf = self.T([128], F32)
            self.ones = self.T([64], F32)
            self.junk = self.T([1024], F32)
            self.ssq_x = [self.T([1], F32) for _ in range(2)]
            self.rs_x = [self.T([1], F32) for _ in range(2)]
            self.base_off = self.off
            identf, ident, ones = self.identf, self.ident, self.ones
            P.add("pool", lambda e: e.memset(identf, 0.0), [], ["identf"])
            P.add("pool", lambda e: e.affine_select(out=identf, in_=identf, pattern=[[-1, 128]],
                                                    compare_op=ALU.not_equal, fill=1.0, base=0,
                                                    channel_multiplier=1), ["identf"], ["identf"])
            P.add("dve", lambda e: e.tensor_copy(out=ident, in_=identf), ["identf"], ["ident"])
            P.add("pool", lambda e: e.memset(ones, 1.0), [], ["ones"])
            P.barrier()
            if "A" in self.phases:
                self.phase_A()
                P.barrier()
            if "B" in self.phases:
                self.phase_B()
                P.barrier()
            if "C" in self.phases:
                self.phase_ffn(0)
                P.barrier()
            if "D" in self.phases:
                self.phase_D()
                P.barrier()
            if "E" in self.phases:
                self.phase_E()
                P.barrier()
            if "F" in self.phases:
                self.phase_ffn(1)
                P.barrier()
            P.emit()
        return nc

    def phase_A(self):
        P = self.P
        ps = self.ps
        self.reset_arena()
        T = self.T
        W1 = T([8, 512], BF16)
        W2 = T([8, 416], BF16)
        W3 = T([8, 512], BF16)
        Wuq = T([3, 768], BF16)
        Wuk = T([2, 512], BF16)
        Wuv = T([2, 512], BF16)
        g0 = T([1024], F32)
        gq = T([64], F32)
        gk = T([64], F32)
        gbq = T([384], F32)
        gbkv = T([256], F32)
        wv = self.w_in.rearrange("(k p) n -> p k n", p=128)
        self.dma(W1[:, :, 0:256], wv[:, :, 512:768], [], ["W1"], "W1a", q="pool")
        self.dma(W1[:, :, 256:512], wv[:, :, 1152:1408], [], ["W1"], "W1a", q="pool")
        self.dma(W2[:, :, 0:32], wv[:, :, 1408:1440], [], ["W2"], "W2a", q="pool")
        self.dma(W2[:, :, 32:416], wv[:, :, 768:1152], [], ["W2"], "W2a", q="pool")
        self.dma(W3, wv[:, :, 0:512], [], ["W3"], "W3a", q="pool")
        self.wload(Wuq, self.w_uq, "Wuq")
        ukv = self.w_ukv.rearrange("(k p) (h two d) -> p k h two d", p=128, two=2, d=64)
        for k in range(2):
            self.dma(Wuk[:, k, :].rearrange("p (h d) -> p h d", d=64), ukv[:, k, :, 0, :], [], ["Wuk"], "Wuk", q="pool")
            self.dma(Wuv[:, k, :].rearrange("p (h d) -> p h d", d=64), ukv[:, k, :, 1, :], [], ["Wuv"], "Wuv", q="pool")
        self.bcast_load(g0, self.norm_mix[0, :], "g0")
        self.bcast_load(gq, self.a_q_norm, "gq")
        self.bcast_load(gk, self.a_k_norm, "gk")
        self.bcast_load(gbq, self.b_q_norm, "gbq")
        self.bcast_load(gbkv, self.b_kv_norm, "gbkv")
        xt = [T([1024], F32) for _ in range(2)]
        tab = [T([192], F32) for _ in range(2)]
        hn = [T([1024], BF16) for _ in range(2)]
        hnT = [T([8, 128], BF16) for _ in range(2)]
        t1 = T([512], F32)
        t2 = T([512], F32)
        sqt = T([512], F32)
        ssq = T([16], F32)
        rs = T([16], F32)
        nrm = T([512], F32)
        kbf = T([128], BF16)
        qbf = T([512], BF16)
        ckvn = T([256], BF16)
        cqn = T([384], BF16)
        krbf = T([128], BF16)
        cqT = T([3, 128], BF16)
        qB = T([8, 128], BF16)
        P.add("pool", lambda e: e.memset(krbf, 0.0), [], ["krbf"])
        P.add("pool", lambda e: e.memset(qB, 0.0), [], ["qB"])
        ckvT = [T([2, 512], BF16) for _ in range(2)]
        kA_st = [T([512], BF16) for _ in range(2)]
        kB_st = [T([8, 512], BF16, parts=64) for _ in range(2)]
        kr_st = [T([512], BF16, parts=32) for _ in range(2)]
        v_st = [T([10, 4, 65], BF16) for _ in range(2)]
        qA_st = [T([4, 512], BF16) for _ in range(2)]
        qB_st = [T([8, 512], BF16, parts=96) for _ in range(2)]
        for gp in range(2):
            vs = v_st[gp]
            P.add("pool", lambda e, vs=vs: e.memset(vs, 1.0), [], ["v_st%d" % gp])
        psb = [ps[:, b, :].bitcast(BF16) for b in range(8)]

        groups = [(t, 4) for t in range(0, 32, 4)] + [(32, 2)] + [(t, 4) for t in range(34, 62, 4)] + [(62, 2)]
        import os
        KSTOP = int(os.environ.get('K_STOP', '99'))
        groups = groups[:int(os.environ.get('K_MAXG', '99'))]
        for gi, (tg, ng) in enumerate(groups):
            gp = gi % 2
            own_g = tg < NT_OWN
            for j in range(min(ng, int(os.environ.get('K_NG', '9')))):
                t = tg + j
                par = t % 2
                own = t < NT_OWN
                xk, tk, hk, hTk = "xt%d" % par, "tab%d" % par, "hn%d" % par, "hnT%d" % par
                self.dma(xt[par], self.x_d[t * 128:(t + 1) * 128, :], [], [xk], xk)
                self.dma(tab[par], self.tab_d[t * 128:(t + 1) * 128, :], [], [tk], tk)
                self.rms_to_T(xt[par], xk, g0, "g0", hn[par], hk, hnT[par], hTk, psb[0], "ps0", par)
                if KSTOP <= 1:
                    continue
                for k in range(8):
                    self.mm(ps[:, 1, :], hnT[par][:, k, :], W1[:, k, :], k == 0, k == 7, [hTk, "W1"], ["ps1"])
                n2 = 416 if own else 32
                for k in range(8):
                    self.mm(ps[:, 2, 0:n2], hnT[par][:, k, :], W2[:, k, 0:n2], k == 0, k == 7, [hTk, "W2"], ["ps2"])
                if own:
                    for k in range(8):
                        self.mm(ps[:, 3, :], hnT[par][:, k, :], W3[:, k, :], k == 0, k == 7, [hTk, "W3"], ["ps3"])
                if KSTOP <= 2:
                    continue
                z1, z2, z3 = ps[:, 1, :], ps[:, 2, :], ps[:, 3, :]
                CA, SA, CB, SB = tab[par][:, 0:64], tab[par][:, 64:128], tab[par][:, 128:160], tab[par][:, 160:192]
                self.copy("act", v_st[gp][:, 0:2, j, 0:64], z1[:, 128:256].rearrange("p (h d) -> p h d", d=64),
                          ["ps1"], ["v_st%d" % gp])
                if KSTOP <= 3:
                    continue
                self.act(sqt[:, 0:128], z1[:, 0:128], AF.Square, ["ps1"], ["sqt"])
                P.add("dve", lambda e: e.tensor_reduce(out=ssq[:, 0:2], in_=sqt[:, 0:128].rearrange("p (h d) -> p h d", d=64),
                                                       axis=AX.X, op=ALU.add), ["sqt"], ["ssq"])
                self.ts("dve", rs[:, 0:2], ssq[:, 0:2], 1.0 / 64, EPS, ALU.mult, ALU.add, ["ssq"], ["rs"])
                self.act(rs[:, 0:2], rs[:, 0:2], AF.Sqrt, ["rs"], ["rs"])
                P.add("dve", lambda e: e.reciprocal(out=rs[:, 0:2], in_=rs[:, 0:2]), ["rs"], ["rs"])
                n3 = nrm[:, 0:128].rearrange("p (h d) -> p h d", d=64)
                self.tt("dve", n3, z1[:, 0:128].rearrange("p (h d) -> p h d", d=64),
                        rs[:, 0:2].unsqueeze(2).to_broadcast([128, 2, 64]), ALU.mult, ["ps1", "rs"], ["nrm"])
                self.tt("pool", n3, n3, gk.unsqueeze(1).to_broadcast([128, 2, 64]), ALU.mult, ["nrm", "gk"], ["nrm"])
                self.rope(kbf.rearrange("p (h d) -> p h d", d=64), n3, CA, SA, 2, 64,
                          t1[:, 0:128].rearrange("p (h d) -> p h d", d=64),
                          t2[:, 0:128].rearrange("p (h d) -> p h d", d=64), [tk], "nrm", "kbf")
                if KSTOP <= 4:
                    continue
                self.act(sqt[:, 0:256], z1[:, 256:512], AF.Square, ["ps1"], ["sqt", "ssq"], accum_out=ssq[:, 2:3])
                self.ts("dve", rs[:, 2:3], ssq[:, 2:3], 1.0 / 256, EPS, ALU.mult, ALU.add, ["ssq"], ["rs"])
                self.act(rs[:, 2:3], rs[:, 2:3], AF.Sqrt, ["rs"], ["rs"])
                P.add("dve", lambda e: e.reciprocal(out=rs[:, 2:3], in_=rs[:, 2:3]), ["rs"], ["rs"])
                self.stt("dve", ckvn, z1[:, 256:512], rs[:, 2:3], gbkv, ALU.mult, ALU.mult, ["ps1", "rs", "gbkv"], ["ckvn"])
                if KSTOP <= 5:
                    continue
                self.copy("act", nrm[:, 0:32], z2[:, 0:32], ["ps2"], ["nrm"])
                self.rope(krbf[:, 0:32].unsqueeze(1), nrm[:, 0:32].unsqueeze(1), CB, SB, 1, 32,
                          t1[:, 0:32].unsqueeze(1), t2[:, 0:32].unsqueeze(1), [tk], "nrm", "krbf", wide=(krbf, t1[:, 0:128]))
                if os.environ.get('K_EXP2'):
                    self.copy('dve', ssq[:, 12:13], rs[:, 12:13], [], ['dummy'])
                if KSTOP <= 6:
                    continue
                self.tr(psb[4][:, 0:128], kbf, ["kbf"], ["ps4"])
                self.tr(psb[4][:, 128:256], ckvn[:, 0:128], ["ckvn"], ["ps4"])
                self.tr(psb[4][:, 256:384], ckvn[:, 128:256], ["ckvn"], ["ps4"])
                self.tr(psb[6][:, 0:128], krbf, ["krbf"], ["ps6"])
                self.copy("act", kA_st[gp][:, j * 128:(j + 1) * 128], psb[4][:, 0:128], ["ps4"], ["kA_st%d" % gp])
                self.copy("dve", ckvT[gp][:, :, j * 128:(j + 1) * 128],
                          psb[4][:, 128:384].rearrange("p (c t) -> p c t", c=2), ["ps4"], ["ckvT%d" % gp])
                self.copy("act", kr_st[gp][:, j * 128:(j + 1) * 128], psb[6][0:32, 0:128], ["ps6"], ["kr_st%d" % gp])
                if KSTOP <= 7:
                    continue
                for k in range(2):
                    self.mm(ps[:, 5, :], ckvT[gp][:, k, j * 128:(j + 1) * 128], Wuv[:, k, :], k == 0, k == 1,
                            ["ckvT%d" % gp, "Wuv"], ["ps5"])
                self.copy("dve", v_st[gp][:, 2:10, j, 0:64], ps[:, 5, :].rearrange("p (h d) -> p h d", d=64),
                          ["ps5"], ["v_st%d" % gp])
                if KSTOP <= 8:
                    continue
                if own:
                    self.act(sqt, z3, AF.Square, ["ps3"], ["sqt"])
                    P.add("dve", lambda e: e.tensor_reduce(out=ssq[:, 4:12], in_=sqt.rearrange("p (h d) -> p h d", d=64),
                                                           axis=AX.X, op=ALU.add), ["sqt"], ["ssq"])
                    self.ts("dve", rs[:, 4:12], ssq[:, 4:12], 1.0 / 64, EPS, ALU.mult, ALU.add, ["ssq"], ["rs"])
                    self.act(rs[:, 4:12], rs[:, 4:12], AF.Sqrt, ["rs"], ["rs"])
                    P.add("dve", lambda e: e.reciprocal(out=rs[:, 4:12], in_=rs[:, 4:12]), ["rs"], ["rs"])
                    n8 = nrm.rearrange("p (h d) -> p h d", d=64)
                    self.tt("dve", n8, z3.rearrange("p (h d) -> p h d", d=64),
                            rs[:, 4:12].unsqueeze(2).to_broadcast([128, 8, 64]), ALU.mult, ["ps3", "rs"], ["nrm"])
                    self.tt("pool", n8, n8, gq.unsqueeze(1).to_broadcast([128, 8, 64]), ALU.mult, ["nrm", "gq"], ["nrm"])
                    self.rope(qbf.rearrange("p (h d) -> p h d", d=64), n8, CA, SA, 8, 64,
                              t1.rearrange("p (h d) -> p h d", d=64), t2.rearrange("p (h d) -> p h d", d=64),
                              [tk], "nrm", "qbf")
                    for c in range(4):
                        self.tr(psb[6][:, c * 128:(c + 1) * 128], qbf[:, c * 128:(c + 1) * 128], ["qbf"], ["ps6"])
                    self.copy("act", qA_st[gp][:, :, j * 128:(j + 1) * 128],
                              psb[6][:, 0:512].rearrange("p (c t) -> p c t", c=4), ["ps6"], ["qA_st%d" % gp])
                    self.act(sqt[:, 0:384], z2[:, 32:416], AF.Square, ["ps2"], ["sqt", "ssq"], accum_out=ssq[:, 3:4])
                    self.ts("dve", rs[:, 3:4], ssq[:, 3:4], 1.0 / 384, EPS, ALU.mult, ALU.add, ["ssq"], ["rs"])
                    self.act(rs[:, 3:4], rs[:, 3:4], AF.Sqrt, ["rs"], ["rs"])
                    P.add("dve", lambda e: e.reciprocal(out=rs[:, 3:4], in_=rs[:, 3:4]), ["rs"], ["rs"])
                    self.stt("dve", cqn, z2[:, 32:416], rs[:, 3:4], gbq, ALU.mult, ALU.mult, ["ps2", "rs", "gbq"], ["cqn"])
                    for c in range(3):
                        self.tr(psb[6][:, c * 128:(c + 1) * 128], cqn[:, c * 128:(c + 1) * 128], ["cqn"], ["ps6"])
                    self.copy("dve", cqT, psb[6][:, 0:384].rearrange("p (c t) -> p c t", c=3), ["ps6"], ["cqT"])
                    for k in range(3):
                        self.mm(ps[:, 7, :], cqT[:, k, :], Wuq[:, k, 0:512], k == 0, k == 2, ["cqT", "Wuq"], ["ps7"])
                    for k in range(3):
                        self.mm(ps[:, 3, 0:256], cqT[:, k, :], Wuq[:, k, 512:768], k == 0, k == 2, ["cqT", "Wuq"], ["ps3"])
                    self.copy("act", nrm, ps[:, 7, :], ["ps7"], ["nrm"])
                    self.copy("act", sqt[:, 0:256], ps[:, 3, 0:256], ["ps3"], ["sqt"])
                    qfull = [None] * 8
                    for hb in range(8):
                        lo = hb * 96
                        if lo + 64 <= 512:
                            self.copy("pool", qB[:, hb, 0:64], nrm[:, lo:lo + 64], ["nrm"], ["qB"])
                        elif lo >= 512:
                            self.copy("pool", qB[:, hb, 0:64], sqt[:, lo - 512:lo - 512 + 64], ["sqt"], ["qB"])
                        else:
                            a = 512 - lo
                            self.copy("pool", qB[:, hb, 0:a], nrm[:, lo:512], ["nrm"], ["qB"])
                            self.copy("pool", qB[:, hb, a:64], sqt[:, 0:64 - a], ["sqt"], ["qB"])
                        r0 = lo + 64
                        if r0 + 32 <= 512:
                            src, sk = nrm[:, r0:r0 + 32], "nrm"
                        else:
                            src, sk = sqt[:, r0 - 512:r0 - 512 + 32], "sqt"
                        self.rope(qB[:, hb:hb + 1, 64:96], src.unsqueeze(1), CB, SB, 1, 32,
                                  t1[:, 0:32].unsqueeze(1), t2[:, 0:32].unsqueeze(1), [tk], sk, "qB")
                    for hb in range(8):
                        self.tr(psb[0][:, hb * 128:(hb + 1) * 128], qB[:, hb, :], ["qB"], ["ps0"])
                    self.copy("act", qB_st[gp][:, :, j * 128:(j + 1) * 128],
                              psb[0][0:96, :].rearrange("p (h t) -> p h t", h=8), ["ps0"], ["qB_st%d" % gp])
            ntok = ng * 128
            t0 = tg * 128
            for hb in range(8):
                bank = 1 + (hb % 3)
                for k in range(2):
                    self.mm(ps[0:64, bank, 0:ntok], Wuk[:, k, hb * 64:(hb + 1) * 64], ckvT[gp][:, k, 0:ntok],
                            k == 0, k == 1, ["ckvT%d" % gp, "Wuk"], ["ps%d" % bank])
                self.copy("act" if hb % 2 else "dve", kB_st[gp][:, hb, 0:ntok], ps[0:64, bank, 0:ntok],
                          ["ps%d" % bank], ["kB_st%d" % gp])
            sk = "stA%d" % gp
            self.dma(self.kT_d[0, :, t0:t0 + ntok], kA_st[gp][0:64, 0:ntok], ["kA_st%d" % gp], [], sk)
            self.dma(self.kT_d[1, :, t0:t0 + ntok], kA_st[gp][64:128, 0:ntok], ["kA_st%d" % gp], [], sk)
            self.dma(self.kT_d[2:10, :, t0:t0 + ntok].rearrange("h d t -> d h t"), kB_st[gp][:, :, 0:ntok],
                     ["kB_st%d" % gp], [], sk)
            self.dma(self.krT_d[:, t0:t0 + ntok], kr_st[gp][:, 0:ntok], ["kr_st%d" % gp], [], sk)
            self.dma(self.vE_d[:, :, tg:tg + ng, :].rearrange("i p t e -> p i t e"), v_st[gp][:, :, 0:ng, :],
                     ["v_st%d" % gp], [], sk)
            if own_g:
                qv = self.qT_d[0:8, 0:64, t0:t0 + ntok].rearrange("(c two) d t -> two d c t", two=2)
                for two in range(2):
                    self.dma(qv[two], qA_st[gp][two * 64:(two + 1) * 64, :, 0:ntok], ["qA_st%d" % gp], [], sk)
                self.dma(self.qT_d[8:16, :, t0:t0 + ntok].rearrange("h d t -> d h t"), qB_st[gp][:, :, 0:ntok],
                         ["qB_st%d" % gp], [], sk)

    def phase_B(self):
        import os
        P, ps, T = self.P, self.ps, self.T
        self.reset_arena()
        kbuf = [T([S_ALL], BF16, parts=96) for _ in range(2)]
        vbuf = [T([NT_ALL, 65], BF16) for _ in range(2)]
        qbuf = [T([TOK_OWN], BF16, parts=96) for _ in range(2)]
        pT = [T([3, 512], BF16) for _ in range(2)]
        rec = T([512], F32)
        osb = T([512], F32)
        mix = [T([512], BF16, parts=64) for _ in range(2)]
        ones = self.ones
        for par in range(2):
            self.dma(kbuf[par][64:96, :], self.krT_d[:, :], [], ["kb%d" % par], "kr%d" % par)
        heads = [int(x) for x in os.environ.get("K_HEADS", ",".join(str(i) for i in range(16))).split(",")]
        ngr = int(os.environ.get("K_BG", "9"))

        def load(hi):
            h = heads[hi]
            par = hi % 2
            idx = h // 4 if h < 8 else h - 6
            dk = 64 if h < 8 else 96
            self.dma(kbuf[par][0:64, :], self.kT_d[idx, :, :], [], ["kb%d" % par], "k%d" % par)
            self.dma(vbuf[par], self.vE_d[idx, :, :, :], [], ["vb%d" % par], "v%d" % par)
            self.dma(qbuf[par][0:dk, :], self.qT_d[h, 0:dk, :], [], ["qb%d" % par], "q%d" % par)

        batches = []
        for hi, h in enumerate(heads):
            for g in range(ngr):
                for kt in range(0, NT_ALL, 3):
                    batches.append((hi, h, g, kt, min(3, NT_ALL - kt)))

        def emit_S(bi):
            hi, h, g, kt, nb = batches[bi]
            par, sb = hi % 2, bi % 2
            dk = 64 if h < 8 else 96
            q0 = g * 512
            nq = 512 if g < 8 else 256
            for i in range(nb):
                self.mm(ps[:, sb * 3 + i, 0:nq], kbuf[par][0:dk, (kt + i) * 128:(kt + i + 1) * 128],
                        qbuf[par][0:dk, q0:q0 + nq], True, True, ["kb%d" % par, "qb%d" % par], ["S%d" % sb])

        def emit_rest(bi):
            hi, h, g, kt, nb = batches[bi]
            par, sb = hi % 2, bi % 2
            dk = 64 if h < 8 else 96
            scale = float(dk) ** -0.5
            nq = 512 if g < 8 else 256
            self.act(pT[sb][:, 0:nb, 0:nq], ps[:, sb * 3:sb * 3 + nb, 0:nq], AF.Exp,
                     ["S%d" % sb], ["pT%d" % sb], scale=scale)
            for i in range(nb):
                self.mm(ps[0:65, 6, 0:nq], vbuf[par][:, kt + i, :], pT[sb][:, i, 0:nq],
                        kt + i == 0, kt + i == NT_ALL - 1, ["vb%d" % par, "pT%d" % sb], ["O"])
            if kt + nb == NT_ALL:
                self.copy("dve", osb[0:65, 0:nq], ps[0:65, 6, 0:nq], ["O"], ["osb"])
                P.add("dve", lambda e, nq=nq: e.reciprocal(out=rec[64:65, 0:nq], in_=osb[64:65, 0:nq]), ["osb"], ["rec"])
                return (h, g, nq)
            return None

        def emit_norm(pend):
            h, g, nq = pend
            q0 = g * 512
            mk = "mix%d" % (g % 2)
            self.mm(ps[0:64, 7, 0:nq], ones[64:65, 0:64], rec[64:65, 0:nq], True, True, ["rec", "ones"], ["bc"])
            self.tt("dve", mix[g % 2][:, 0:nq], osb[0:64, 0:nq], ps[0:64, 7, 0:nq], ALU.mult, ["osb", "bc"], [mk])
            self.dma(self.mix0_d[h, :, q0:q0 + nq], mix[g % 2][:, 0:nq], [mk], [], "mo%d" % (g % 2))

        load(0)
        emit_S(0)
        pend = None
        for bi in range(len(batches)):
            hi, h, g, kt, nb = batches[bi]
            if g == 0 and kt == 0 and hi + 1 < len(heads):
                load(hi + 1)
            if bi + 1 < len(batches):
                emit_S(bi + 1)
            newp = emit_rest(bi)
            if pend is not None:
                emit_norm(pend)
                pend = None
            if newp is not None:
                pend = newp
        if pend is not None:
            emit_norm(pend)

    def phase_ffn(self, layer):
        import os
        P, ps, T = self.P, self.ps, self.T
        self.reset_arena()
        L = layer
        wup = T([8, 4096], BF16)
        wdn = T([32, 1024], BF16)
        wo = T([8, 1024], BF16)
        gf = T([1024], F32)
        self.wload(wo, self.w_out0 if L == 0 else self.w_out1, "wo")
        upv = self.w_up[L].rearrange("(k p) n -> p k n", p=128)
        for k in range(8):
            self.dma(wup[:, k, :], upv[:, k, :], [], ["wup"], "wup", q="pool")
        dnv = self.w_down[L].rearrange("(k p) n -> p k n", p=128)
        for k4 in range(8):
            self.dma(wdn[:, k4 * 4:(k4 + 1) * 4, :], dnv[:, k4 * 4:(k4 + 1) * 4, :], [], ["wdn"], "wdn", q="pool")
        self.bcast_load(gf, self.norm_ffn[L, :], "gf")
        if L == 1:
            gfin = T([1024], F32)
            self.bcast_load(gfin, self.final_norm, "gfin")
        mixg = T([8, 256], BF16)
        xt = [T([1024], F32) for _ in range(2)]
        h1 = T([2, 1024], F32)
        hn = T([1024], BF16)
        hnT = T([8, 256], BF16)
        uT = T([32, 256], BF16)
        rl = [T([256], F32) for _ in range(2)]
        psb0 = ps[:, 0, :].bitcast(BF16)
        mix_d = self.mix0_d if L == 0 else self.mix1_d
        res_d = self.x_d if L == 0 else self.h1_d
        ngroups = (17 if L == 0 else 16)
        ngroups = min(ngroups, int(os.environ.get("K_FG", "99")))
        junk = self.junk
        for g in range(ngroups):
            t0 = g * 256
            mv = mix_d[:, :, t0:t0 + 256].rearrange("(c two) d t -> (two d) c t", two=2)
            self.dma(mixg, mv, [], ["mixg"], "mixg")
            for j in range(2):
                tok = t0 + j * 128
                hk = "h1_%d" % j
                self.dma(xt[j], res_d[tok:tok + 128, :], [], ["xt%d" % j], "xt%d" % j)
                for half in range(2):
                    hs = slice(half * 512, (half + 1) * 512)
                    for c in range(8):
                        self.mm(ps[:, 1 + half, :], mixg[:, c, j * 128:(j + 1) * 128], wo[:, c, hs],
                                c == 0, c == 7, ["mixg", "wo"], ["ps%d" % (1 + half)])
                    self.tt("dve", h1[:, j, hs], ps[:, 1 + half, :], xt[j][:, hs], ALU.add,
                            ["ps%d" % (1 + half), "xt%d" % j], [hk])
                self.rms_to_T(h1[:, j, :], hk, gf, "gf", hn, "hn", hnT[:, :, j * 128:(j + 1) * 128], "hnT",
                              psb0, "ps0", j)
            for f in range(32):
                bank = 3 + (f % 2)
                rk = "rl%d" % (f % 2)
                for k in range(8):
                    self.mm(ps[:, bank, 0:256], wup[:, k, f * 128:(f + 1) * 128], hnT[:, k, :], k == 0, k == 7,
                            ["wup", "hnT"], ["ps%d" % bank])
                self.act(rl[f % 2], ps[:, bank, 0:256], AF.Relu, ["ps%d" % bank], [rk])
                self.tt("dve" if f % 2 == 0 else "pool", uT[:, f, :], rl[f % 2], rl[f % 2], ALU.mult,
                        [rk], ["uT%d" % (f % 2)])
            for j in range(2):
                tok = t0 + j * 128
                hk = "h1_%d" % j
                for half in range(2):
                    hs = slice(half * 512, (half + 1) * 512)
                    bank = 5 + half
                    for f in range(32):
                        self.mm(ps[:, bank, :], uT[:, f, j * 128:(j + 1) * 128], wdn[:, f, hs], f == 0, f == 31,
                                ["uT0", "uT1", "wdn"], ["ps%d" % bank])
                    self.tt("dve", h1[:, j, hs], ps[:, bank, :], h1[:, j, hs], ALU.add, ["ps%d" % bank, hk], [hk])
                if L == 0:
                    self.dma(self.h1_d[tok:tok + 128, :], h1[:, j, :], [hk], [], "h1o%d" % j)
                else:
                    ssq, rs = self.ssq_x[j], self.rs_x[j]
                    sk, rk2 = "ssqx%d" % j, "rsx%d" % j
                    self.act(junk, h1[:, j, :], AF.Square, [hk], ["junk", sk], accum_out=ssq)
                    self.ts("dve", rs, ssq, 1.0 / 1024, EPS, ALU.mult, ALU.add, [sk], [rk2])
                    self.act(rs, rs, AF.Sqrt, [rk2], [rk2])
                    P.add("dve", lambda e, rs=rs: e.reciprocal(out=rs, in_=rs), [rk2], [rk2])
                    self.stt("dve", xt[j], h1[:, j, :], rs, gfin, ALU.mult, ALU.mult, [hk, rk2, "gfin"], ["xt%d" % j])
                    self.dma(self.out_d[tok:tok + 128, :], xt[j], ["xt%d" % j], [], "outo%d" % j)

    def phase_D(self):
        P, ps, T = self.P, self.ps, self.T
        self.reset_arena()
        wq = T([8, 3072], BF16)
        wv = self.w_qkv.rearrange("(k p) n -> p k n", p=128)
        for i in range(6):
            self.dma(wq[:, :, i * 512:(i + 1) * 512], wv[:, :, i * 512:(i + 1) * 512], [], ["wq"], "wq", q="pool")
        g1 = T([1024], F32)
        self.bcast_load(g1, self.norm_mix[1, :], "g1")
        xt = [T([1024], F32) for _ in range(2)]
        hn = [T([1024], BF16) for _ in range(2)]
        hnT = T([8, 512], BF16)
        qk_st = T([16, 512], BF16)
        v_st = [T([16, 65], BF16) for _ in range(2)]
        for i in range(2):
            vs = v_st[i]
            P.add("pool", lambda e, vs=vs: e.memset(vs, 1.0), [], ["v1st%d" % i])
        psb0 = ps[:, 0, :].bitcast(BF16)
        groups = [(t, 4) for t in range(0, 32, 4)] + [(32, 2)]
        for tg, ng in groups:
            ntok = ng * 128
            t0 = tg * 128
            for j in range(ng):
                t = tg + j
                par = t % 2
                self.dma(xt[par], self.h1_d[t * 128:(t + 1) * 128, :], [], ["xt%d" % par], "xt%d" % par)
                self.rms_to_T(xt[par], "xt%d" % par, g1, "g1", hn[par], "hn%d" % par,
                              hnT[:, :, j * 128:(j + 1) * 128], "hnT", psb0, "ps0", par)
            for cc in range(16):
                bank = 1 + (cc % 2)
                col = cc * 128
                for k in range(8):
                    self.mm(ps[:, bank, 0:ntok], wq[:, k, col:col + 128], hnT[:, k, 0:ntok], k == 0, k == 7,
                            ["wq", "hnT"], ["ps%d" % bank])
                self.copy("act" if cc % 2 else "dve", qk_st[:, cc, 0:ntok], ps[:, bank, 0:ntok],
                          ["ps%d" % bank], ["qk_st"])
            self.dma(self.q1T_d[:, :, t0:t0 + ntok].rearrange("c p t -> p c t"), qk_st[:, 0:8, 0:ntok],
                     ["qk_st"], [], "qko")
            self.dma(self.k1T_d[:, :, t0:t0 + ntok].rearrange("c p t -> p c t"), qk_st[:, 8:16, 0:ntok],
                     ["qk_st"], [], "qko")
            for j in range(ng):
                t = tg + j
                par = t % 2
                for half in range(2):
                    bank = 3 + half
                    for k in range(8):
                        self.mm(ps[:, bank, :], hnT[:, k, j * 128:(j + 1) * 128],
                                wq[:, k, 2048 + half * 512:2048 + (half + 1) * 512], k == 0, k == 7,
                                ["wq", "hnT"], ["ps%d" % bank])
                    self.copy("act" if half else "dve", v_st[par][:, half * 8:(half + 1) * 8, 0:64],
                              ps[:, bank, :].rearrange("p (h d) -> p h d", d=64), ["ps%d" % bank], ["v1st%d" % par])
                self.dma(self.v1_d[:, t, :, :], v_st[par], ["v1st%d" % par], [], "v1o%d" % par)

    def phase_E(self):
        import os
        P, ps, T = self.P, self.ps, self.T
        self.reset_arena()
        k1T = T([8, TOK_OWN], BF16)
        v1f = T([NT_OWN * 16 * 65], BF16)
        v1 = v1f.rearrange("p (t h e) -> p t h e", t=NT_OWN, h=16, e=65)
        bias = T([16, 640], F32)
        qt = [T([8, 128], BF16) for _ in range(2)]
        pT = [T([2, 640], BF16) for _ in range(2)]
        rec2 = [T([256], F32) for _ in range(2)]
        osb2 = [T([256], F32) for _ in range(2)]
        mst = [T([16, 128], BF16, parts=64) for _ in range(2)]
        ones = self.ones
        self.dma(k1T, self.k1T_d.rearrange("c p t -> p c t"), [], ["k1T"], "k1T")
        self.dma(v1f, self.v1_d.rearrange("p t h e -> p (t h e)"), [], ["v1"], "v1")
        ntiles = min(NT_OUT, int(os.environ.get("K_ET", "99")))
        pairs = [(t, c) for t in range(ntiles) for c in range(8)]

        def loadq(t):
            par = t % 2
            self.dma(qt[par], self.q1T_d[:, :, t * 128:(t + 1) * 128].rearrange("c p t -> p c t"), [],
                     ["qt%d" % par], "qt%d" % par)

        def emit_S(pi):
            t, c = pairs[pi]
            par, sb = t % 2, pi % 2
            ws = max(t - 2, 0)
            for two in range(2):
                pb = two * 64
                for j in range(5):
                    col = two * 640 + j * 128
                    self.mm(ps[:, sb * 3 + col // 512, col % 512:col % 512 + 128],
                            k1T[pb:pb + 64, c, (ws + j) * 128:(ws + j + 1) * 128], qt[par][pb:pb + 64, c, :],
                            True, True, ["k1T", "qt%d" % par], ["S%d" % sb])

        def emit_rest(pi):
            t, c = pairs[pi]
            par, sb = t % 2, pi % 2
            ws = max(t - 2, 0)
            sv = ps[:, sb * 3:sb * 3 + 3, :].rearrange("p b n -> p (b n)")[:, 0:1280]
            self.stt("dve", sv, sv, 0.125, bias[:, 2 * c:2 * c + 2, :].rearrange("p h n -> p (h n)"),
                     ALU.mult, ALU.add, ["S%d" % sb, "bias"], ["S%d" % sb])
            self.act(pT[sb].rearrange("p a n -> p (a n)"), sv, AF.Exp, ["S%d" % sb], ["pT%d" % sb])
            for two in range(2):
                h = 2 * c + two
                for j in range(5):
                    self.mm(ps[0:65, 6, two * 128:(two + 1) * 128], v1[:, ws + j, h, :], pT[sb][:, two, j * 128:(j + 1) * 128],
                            j == 0, j == 4, ["v1", "pT%d" % sb], ["O"])
            osb, rec = osb2[pi % 2], rec2[pi % 2]
            self.copy("dve", osb[0:65, :], ps[0:65, 6, 0:256], ["O"], ["osb%d" % (pi % 2)])
            P.add("dve", lambda e: e.reciprocal(out=rec[64:65, :], in_=osb[64:65, :]), ["osb%d" % (pi % 2)], ["rec%d" % (pi % 2)])

        def emit_norm(pi):
            t, c = pairs[pi]
            par = t % 2
            osb, rec = osb2[pi % 2], rec2[pi % 2]
            self.mm(ps[0:64, 7, 0:256], ones[64:65, 0:64], rec[64:65, :], True, True, ["rec%d" % (pi % 2), "ones"], ["bc"])
            self.tt("dve", mst[par][:, 2 * c:2 * c + 2, :], osb[0:64, :].rearrange("p (a q) -> p a q", a=2),
                    ps[0:64, 7, 0:256].rearrange("p (a q) -> p a q", a=2), ALU.mult, ["osb%d" % (pi % 2), "bc"], ["mst%d" % par])
            if c == 7:
                self.dma(self.mix1_d[:, :, t * 128:(t + 1) * 128].rearrange("h d t -> d h t"), mst[par],
                         ["mst%d" % par], [], "m1o%d" % par)

        loadq(0)
        self.dma(bias, self.biasT[0, :, :, :], [], ["bias"], "bias")
        emit_S(0)
        pend = None
        for pi in range(len(pairs)):
            t, c = pairs[pi]
            if c == 0 and t + 1 < ntiles:
                loadq(t + 1)
            nxt = pi + 1
            if nxt < len(pairs):
                tn, cn = pairs[nxt]
                if cn == 0 and tn <= 2:
                    emit_rest(pi)
                    if pend is not None:
                        emit_norm(pend)
                    pend = pi
                    self.dma(bias, self.biasT[tn, :, :, :], [], ["bias"], "bias")
                    emit_S(nxt)
                    continue
                emit_S(nxt)
            emit_rest(pi)
            if pend is not None:
                emit_norm(pend)
            pend = pi
        if pend is not None:
            emit_norm(pend)


def _perm(half):
    rows = np.arange(128) if half == 0 else np.arange(127, -1, -1)
    return (rows[:, None] * 64 + np.arange(64)[None, :]).reshape(-1)


def _tables(perm):
    row = (perm // 64).astype(np.float32)
    col = (perm % 64).astype(np.float32)

    def cs(pos, d):
        inv = (np.float32(10000.0) ** (-np.arange(0, d, 2, dtype=np.float32) / np.float32(d))).astype(np.float32)
        ang = (pos[:, None] * inv[None, :]).astype(np.float32)
        return np.cos(ang).astype(np.float32), np.sin(ang).astype(np.float32)

    out = []
    for d in (32, 16):
        cr, sr = cs(row, d)
        cc, sc = cs(col, d)
        out.append(np.concatenate([cr, cr, cc, cc], axis=1))
        out.append(np.concatenate([-sr, sr, -sc, sc], axis=1))
    return np.ascontiguousarray(np.concatenate(out, axis=1), dtype=np.float32)


def _bias_tables(rpb, half):
    out = np.full((3, 128, 16, 5, 128), NEG, dtype=np.float32)
    qi = np.arange(128)
    for v in range(3):
        t = v
        ws = max(t - 2, 0)
        lrq = 2 * t + qi // 64
        cq = qi % 64
        grq = lrq if half == 0 else 127 - lrq
        rs_ = np.clip(grq - 4, 0, 120)
        cs_ = np.clip(cq - 8, 0, 48)
        for j in range(5):
            kp = np.arange(128)
            lrk = 2 * (ws + j) + kp // 64
            ck = kp % 64
            grk = lrk if half == 0 else 127 - lrk
            valid = ((grk[:, None] >= rs_[None, :]) & (grk[:, None] < rs_[None, :] + 8) &
                     (ck[:, None] >= cs_[None, :]) & (ck[:, None] < cs_[None, :] + 16))
            ridx = np.clip(grk[:, None] - grq[None, :] + 7, 0, 14)
            cidx = np.clip(ck[:, None] - cq[None, :] + 15, 0, 30)
            g = rpb[:, ridx, cidx]
            g = np.where(valid[None], g, np.float32(NEG))
            out[v, :, :, j, :] = g.transpose(1, 0, 2)
    return np.ascontiguousarray(out.reshape(3, 128, 16, 640))


_CACHE = {}


def make_in_maps(x, norm_mix, ev_w_in, ev_a_q_norm, ev_a_k_norm, ev_b_q_norm, ev_b_w_uq,
                 ev_b_kv_norm, ev_b_w_ukv, ev_w_out, od_w_qkv, od_rpb, od_w_out,
                 norm_ffn, ffn_w_up, ffn_w_down, final_norm):
    f = lambda a: np.ascontiguousarray(np.asarray(a), dtype=np.float32)
    shared = {
        "norm_mix": f(norm_mix), "norm_ffn": f(norm_ffn), "final_norm": f(final_norm),
        "w_in": f(ev_w_in)[0], "a_q_norm": f(ev_a_q_norm)[0], "a_k_norm": f(ev_a_k_norm)[0],
        "b_q_norm": f(ev_b_q_norm)[0], "b_kv_norm": f(ev_b_kv_norm)[0], "w_uq": f(ev_b_w_uq)[0],
        "w_ukv": f(ev_b_w_ukv)[0], "w_out0": f(ev_w_out)[0], "w_qkv": f(od_w_qkv)[0],
        "w_out1": f(od_w_out)[0], "w_up": f(ffn_w_up), "w_down": f(ffn_w_down),
    }
    x = f(x)
    rpb = f(od_rpb)[0]
    perms = [_perm(0), _perm(1)]
    tabs = [_tables(p) for p in perms]
    biases = [_bias_tables(rpb, h) for h in range(2)]
    in_maps = []
    for c in range(8):
        b, h = c // 2, c % 2
        m = dict(shared)
        m["x"] = np.ascontiguousarray(x[b][perms[h]])
        m["tab"] = tabs[h]
        m["biasT"] = biases[h]
        in_maps.append(m)
    return in_maps, perms


def kernel(**inputs):
    in_maps, perms = make_in_maps(**inputs)
    if "nc" not in _CACHE:
        _CACHE["nc"] = Builder(debug=False).build()
    res = run_bass_kernel_spmd(_CACHE["nc"], in_maps, core_ids=list(range(8)))
    out = np.empty((4, S_ALL, 1024), dtype=np.float32)
    for c in range(8):
        b, h = c // 2, c % 2
        out[b][perms[h][:TOK_OUT]] = res.results[c]["out"]
    return out
```
